# Optimizing a Trainium2 kernel written in Bass

```python
import jax, jax.numpy as jnp
from jax import lax
import numpy as np

D_MODEL = 1024
BATCH = 16
SEQ = 2048
DEPTH = 2
DEC_BATCH = 8
DEC_SEQ = 64
PAST_LEN = 4096

CHUNK = 64
N_A = DEPTH // 2
N_B = DEPTH - N_A
K_A = 128
H_A = D_MODEL // K_A
V_A = D_MODEL // H_A
D_A = H_A * K_A
H_B = 16
D_HB = D_MODEL // H_B
D_FF = 4 * D_MODEL
Q_BLOCK = 128
EPS = 1e-6

kernel_name = "yoco_hgrn2_fox_stream_step"


def _rmsnorm(x, g):
    xf = x.astype(jnp.float32)
    y = xf * lax.rsqrt(jnp.mean(xf * xf, axis=-1, keepdims=True) + EPS)
    return (y * g.astype(jnp.float32)).astype(x.dtype)


def _mlp(xn, w_up, w_down):
    h = jax.nn.relu(xn @ w_up)
    return (h * h) @ w_down


def _gla_chunked(q, k, v, logf, s0, chunk):
    B, T, H, K = q.shape
    V = v.shape[-1]
    N = T // chunk
    q = q.reshape(B, N, chunk, H, K)
    k = k.reshape(B, N, chunk, H, K)
    v = v.reshape(B, N, chunk, H, V)
    bc = jnp.cumsum(logf.reshape(B, N, chunk, H, K), axis=2)
    btot = bc[:, :, -1]
    q_dec = q * jnp.exp(bc)
    k_inv = k * jnp.exp(-bc)
    k_end = k * jnp.exp(btot[:, :, None] - bc)
    mask = jnp.tril(jnp.ones((chunk, chunk), dtype=bool))
    scores = jnp.einsum('bnthk,bnshk->bnhts', q_dec, k_inv)
    scores = jnp.where(mask, scores, 0.0)
    o_intra = jnp.einsum('bnhts,bnshv->bnthv', scores, v)
    u = jnp.einsum('bnshk,bnshv->bnhkv', k_end, v)
    decay = jnp.exp(btot)

    def step(S, inp):
        d, un = inp
        return d[..., None] * S + un, S

    s_fin, s_before = lax.scan(step, s0, (jnp.moveaxis(decay, 1, 0), jnp.moveaxis(u, 1, 0)))
    s_before = jnp.moveaxis(s_before, 0, 1)
    o_inter = jnp.einsum('bnthk,bnhkv->bnthv', q_dec, s_before)
    return (o_intra + o_inter).reshape(B, T, H, V), s_fin


def _hgrn2(xn, w_in, lb, g_norm, w_o, s0, chunk):
    B, T, _ = xn.shape
    p = (xn @ w_in).astype(jnp.float32)
    q, zf, i, g = jnp.split(p, [D_A, 2 * D_A, 3 * D_A], axis=-1)
    lb = lb.reshape(H_A, K_A)
    q = jax.nn.silu(q).reshape(B, T, H_A, K_A)
    zf = zf.reshape(B, T, H_A, K_A)
    f = lb + (1.0 - lb) * jax.nn.sigmoid(zf)
    logf = jnp.log(f)
    k = (1.0 - lb) * jax.nn.sigmoid(-zf)
    i = i.reshape(B, T, H_A, V_A)
    o, s_fin = _gla_chunked(q, k, i, logf, s0, chunk)
    o = o * lax.rsqrt(jnp.mean(o * o, axis=-1, keepdims=True) + EPS) * g_norm.astype(jnp.float32)
    o = o * jax.nn.sigmoid(g.reshape(B, T, H_A, V_A))
    return o.reshape(B, T, H_A * V_A).astype(xn.dtype) @ w_o, s_fin


def _shared_kv(h, norm_kv, w_kv, b_f):
    B, T, _ = h.shape
    p = _rmsnorm(h, norm_kv) @ w_kv
    hd = H_B * D_HB
    k = p[..., :hd].reshape(B, T, H_B, D_HB)
    v = p[..., hd:2 * hd].reshape(B, T, H_B, D_HB)
    logf = jax.nn.log_sigmoid(p[..., 2 * hd:].astype(jnp.float32) + b_f.astype(jnp.float32))
    return k, v, logf


def _fox_attend(q, k, v, F_q, F_k, q_pos, k_pos):
    B, T, H, Dh = q.shape
    qb_len = min(Q_BLOCK, T)
    nb = T // qb_len
    scale = Dh ** -0.5
    qb = jnp.moveaxis(q.reshape(B, nb, qb_len, H, Dh), 1, 0)
    Fb = jnp.moveaxis(F_q.reshape(B, nb, qb_len, H), 1, 0)
    pb = q_pos.reshape(nb, qb_len)
    kf = k.astype(jnp.float32)
    vf = v.astype(jnp.float32)
    Fk_t = jnp.swapaxes(F_k, 1, 2)

    def block(args):
        qi, Fi, pi = args
        s = jnp.einsum('bqhd,bkhd->bhqk', qi.astype(jnp.float32), kf) * scale
        s = s + jnp.swapaxes(Fi, 1, 2)[..., :, None] - Fk_t[:, :, None, :]
        mask = k_pos[None, :] <= pi[:, None]
        s = jnp.where(mask[None, None], s, -jnp.inf)
        pr = jax.nn.softmax(s, axis=-1)
        return jnp.einsum('bhqk,bkhd->bqhd', pr, vf)

    o = lax.map(block, (qb, Fb, pb))
    return jnp.moveaxis(o, 0, 1).reshape(B, T, H, Dh)


def _fox(xn, w_q, w_o, k_all, v_all, F_q, F_k, q_pos, k_pos):
    B, T, _ = xn.shape
    hd = H_B * D_HB
    p = xn @ w_q
    q = p[..., :hd].reshape(B, T, H_B, D_HB)
    g = p[..., hd:].astype(jnp.float32)
    o = _fox_attend(q, k_all, v_all, F_q, F_k, q_pos, k_pos).reshape(B, T, hd)
    o = o * jax.nn.sigmoid(g)
    return o.astype(xn.dtype) @ w_o


def _trunk(x, state0, k_past, v_past, logf_past, prm):
    B, T, _ = x.shape
    chunk = min(CHUNK, T)
    lb_all = jnp.cumsum(jax.nn.softmax(prm['lb_logits'].astype(jnp.float32), axis=0), axis=0)
    states = []
    for l in range(N_A):
        if state0 is None:
            s0 = jnp.zeros((B, H_A, K_A, V_A), jnp.float32)
        else:
            s0 = state0[:, l].astype(jnp.float32)
        o, s_fin = _hgrn2(_rmsnorm(x, prm['norm_a'][l]), prm['w_in_a'][l], lb_all[l],
                          prm['g_norm_a'][l], prm['w_o_a'][l], s0, chunk)
        x = x + o
        x = x + _mlp(_rmsnorm(x, prm['norm_mlp'][l]), prm['w_up'][l], prm['w_down'][l])
        states.append(s_fin)
    k_new, v_new, logf_new = _shared_kv(x, prm['norm_kv'], prm['w_kv'], prm['b_f'])
    if k_past is None:
        P = 0
        k_all, v_all, logf_all = k_new, v_new, logf_new
    else:
        P = k_past.shape[1]
        k_all = jnp.concatenate([k_past.astype(k_new.dtype), k_new], axis=1)
        v_all = jnp.concatenate([v_past.astype(v_new.dtype), v_new], axis=1)
        logf_all = jnp.concatenate([logf_past.astype(jnp.float32), logf_new], axis=1)
    F = jnp.cumsum(logf_all, axis=1)
    F_q = F[:, P:]
    k_pos = jnp.arange(P + T, dtype=jnp.int32)
    q_pos = P + jnp.arange(T, dtype=jnp.int32)
    for j in range(N_B):
        l = N_A + j
        x = x + _fox(_rmsnorm(x, prm['norm_b'][j]), prm['w_q_b'][j], prm['w_o_b'][j],
                     k_all, v_all, F_q, F, q_pos, k_pos)
        x = x + _mlp(_rmsnorm(x, prm['norm_mlp'][l]), prm['w_up'][l], prm['w_down'][l])
    y = _rmsnorm(x, prm['norm_f'])
    return y, jnp.stack(states, axis=1), k_new, v_new, logf_new


def setup_inputs(seed: int = 0) -> dict:
    key = jax.random.key(seed)
    ks = jax.random.split(key, 24)

    def nrm(k, shape, scale):
        return jax.random.normal(k, shape, jnp.float32) * scale

    b_f = jnp.linspace(1.0, 5.0, H_B, dtype=jnp.float32) + nrm(ks[0], (H_B,), 0.1)
    hd = H_B * D_HB
    return {
        'x_prompt': nrm(ks[1], (BATCH, SEQ, D_MODEL), 1.0),
        'x_sample': nrm(ks[2], (DEC_BATCH, DEC_SEQ, D_MODEL), 1.0),
        'state_hgrn': nrm(ks[3], (DEC_BATCH, N_A, H_A, K_A, V_A), 0.1),
        'cache_k': nrm(ks[4], (DEC_BATCH, PAST_LEN, H_B, D_HB), 1.0),
        'cache_v': nrm(ks[5], (DEC_BATCH, PAST_LEN, H_B, D_HB), 1.0),
        'cache_logf': jax.nn.log_sigmoid(b_f + nrm(ks[6], (DEC_BATCH, PAST_LEN, H_B), 1.0)),
        'norm_a': 1.0 + nrm(ks[7], (N_A, D_MODEL), 0.02),
        'w_in_a': nrm(ks[8], (N_A, D_MODEL, 4 * D_A), D_MODEL ** -0.5),
        'lb_logits': nrm(ks[9], (N_A + 1, D_A), 0.1),
        'g_norm_a': 1.0 + nrm(ks[10], (N_A, V_A), 0.02),
        'w_o_a': nrm(ks[11], (N_A, H_A * V_A, D_MODEL), (H_A * V_A) ** -0.5),
        'norm_kv': 1.0 + nrm(ks[12], (D_MODEL,), 0.02),
        'w_kv': nrm(ks[13], (D_MODEL, 2 * hd + H_B), D_MODEL ** -0.5),
        'b_f': b_f,
        'norm_b': 1.0 + nrm(ks[14], (N_B, D_MODEL), 0.02),
        'w_q_b': nrm(ks[15], (N_B, D_MODEL, 2 * hd), D_MODEL ** -0.5),
        'w_o_b': nrm(ks[16], (N_B, hd, D_MODEL), hd ** -0.5),
        'norm_mlp': 1.0 + nrm(ks[17], (DEPTH, D_MODEL), 0.02),
        'w_up': nrm(ks[18], (DEPTH, D_MODEL, D_FF), D_MODEL ** -0.5),
        'w_down': nrm(ks[19], (DEPTH, D_FF, D_MODEL), D_FF ** -0.5),
        'norm_f': 1.0 + nrm(ks[20], (D_MODEL,), 0.02),
    }


def reference(x_prompt, x_sample, state_hgrn, cache_k, cache_v, cache_logf,
              norm_a, w_in_a, lb_logits, g_norm_a, w_o_a,
              norm_kv, w_kv, b_f, norm_b, w_q_b, w_o_b,
              norm_mlp, w_up, w_down, norm_f):
    prm = {
        'norm_a': norm_a, 'w_in_a': w_in_a, 'lb_logits': lb_logits, 'g_norm_a': g_norm_a,
        'w_o_a': w_o_a, 'norm_kv': norm_kv, 'w_kv': w_kv, 'b_f': b_f, 'norm_b': norm_b,
        'w_q_b': w_q_b, 'w_o_b': w_o_b, 'norm_mlp': norm_mlp, 'w_up': w_up,
        'w_down': w_down, 'norm_f': norm_f,
    }
    y_prompt, st_p, k_p, v_p, lf_p = _trunk(x_prompt, None, None, None, None, prm)
    y_sample, st_s, k_s, v_s, lf_s = _trunk(x_sample, state_hgrn, cache_k, cache_v, cache_logf, prm)
    return (y_prompt, y_sample, st_p, k_p, v_p, lf_p, st_s, k_s, v_s, lf_s)
```

```python
import numpy as np
from contextlib import ExitStack
import concourse.bass as bass
import concourse.mybir as mybir
from concourse.bass_utils import run_bass_kernel_spmd

F32 = mybir.dt.float32
BF16 = mybir.dt.bfloat16
ALU = mybir.AluOpType
AF = mybir.ActivationFunctionType

COMPUTE = ("pe", "act", "dve", "pool")
QUEUES = ("pe", "act", "dve", "pool", "sp")

D = 1024
NCH = 8
TP = 512
SEQ = 2048
NPS = 2
TS = 64
PAST = 4096
NTOK = NPS * SEQ + TS
EPS = 1e-6
NWT = 52
NW = 4
NEG = -30000.0


class Op:
    __slots__ = ("eng", "fn", "reads", "writes", "lane", "inc", "deps", "sig", "val", "waits", "idx", "tag")

    def __init__(self, eng, fn, reads, writes, lane):
        self.eng = eng
        self.fn = fn
        self.reads = reads
        self.writes = writes
        self.lane = lane
        self.inc = 1 if lane in COMPUTE else 16
        self.deps = []
        self.sig = False
        self.val = None
        self.waits = []


class Prog:
    def __init__(self):
        self.ops = []
        self.tag = ""

    def op(self, eng, fn, reads=(), writes=()):
        o = Op(eng, fn, tuple(reads), tuple(writes), eng)
        o.tag = self.tag
        self.ops.append(o)
        return o

    def dma(self, queue, lane, fn, reads=(), writes=()):
        o = Op(queue, fn, tuple(reads), tuple(writes), ("dma", lane))
        o.tag = self.tag
        self.ops.append(o)
        return o

    def analyze(self):
        last_w = {}
        readers = {}
        last_on_lane = {}
        for i, o in enumerate(self.ops):
            o.idx = i
            deps = {}

            def add(d):
                if d is o:
                    return
                if d.lane == "pe" and o.lane == "pe":
                    return
                deps[d.idx] = d

            for k in o.reads:
                w = last_w.get(k)
                if w is not None:
                    add(w)
            for k in o.writes:
                w = last_w.get(k)
                if w is not None and not (w.lane == o.lane and o.lane in COMPUTE):
                    add(w)
                for r in readers.get(k, {}).values():
                    if not (r.lane == o.lane and o.lane in COMPUTE):
                        add(r)
            if o.lane not in COMPUTE:
                p = last_on_lane.get(o.lane)
                if p is not None:
                    add(p)
                last_on_lane[o.lane] = o
            for k in o.reads:
                readers.setdefault(k, {})[o.lane] = o
            for k in o.writes:
                last_w[k] = o
                readers[k] = {}
            o.deps = list(deps.values())
        ordc = {}
        for o in self.ops:
            ordc[o.lane] = ordc.get(o.lane, 0) + 1
            o.val = ordc[o.lane]
            o.sig = o.lane not in COMPUTE
        known = {q: {} for q in QUEUES}
        clocks = {}
        wait_ops = {}
        nw = 0
        for o in self.ops:
            kn = known[o.eng]
            wl = []
            for d in sorted(o.deps, key=lambda d: -d.idx):
                if kn.get(d.lane, 0) >= d.val:
                    continue
                wl.append(d)
                d.sig = True
                nw += 1
                for l, v in clocks[d.idx].items():
                    if kn.get(l, 0) < v:
                        kn[l] = v
            wait_ops[o.idx] = wl
            c = dict(kn)
            c[o.lane] = o.val
            clocks[o.idx] = c
        cnt = {}
        for o in self.ops:
            if o.sig:
                cnt[o.lane] = cnt.get(o.lane, 0) + o.inc
                o.val = cnt[o.lane]
            else:
                o.val = None
        for o in self.ops:
            o.waits = [(d.lane, d.val) for d in wait_ops[o.idx]]
        self.n_waits = nw

    def lanes(self):
        ls = []
        seen = set()
        for o in self.ops:
            if o.sig and o.lane not in seen:
                seen.add(o.lane)
                ls.append(o.lane)
        return ls

    def emit(self, sems, block):
        per = {q: [] for q in QUEUES}
        for o in self.ops:
            per[o.eng].append(o)

        def run(e, ops):
            for o in ops:
                for (l, v) in o.waits:
                    e.wait_ge(sems[l], v)
                ins = o.fn(e)
                if o.sig:
                    ins.then_inc(sems[o.lane], o.inc)

        block.tensor(lambda e: run(e, per["pe"]))
        block.scalar(lambda e: run(e, per["act"]))
        block.vector(lambda e: run(e, per["dve"]))
        block.gpsimd(lambda e: run(e, per["pool"]))
        block.sync(lambda e: run(e, per["sp"]))


def build(tile_limit=None, phase_limit=99):
    nc = bass.Bass("TRN2", target_bir_lowering=False)
    es = ExitStack()
    P = Prog()

    def dram_in(name, shape, dt=F32):
        return nc.dram_tensor(name, shape, dt, kind="ExternalInput").ap()

    def dram_out(name, shape, dt=F32):
        return nc.dram_tensor(name, shape, dt, kind="ExternalOutput").ap()

    xT = dram_in("xT", [D, NTOK])
    wsrc = dram_in("wsrc", [NWT, 128, 4096])
    wf_src = dram_in("wf", [128, 128])
    vec_src = dram_in("vecs", [128, 65])
    bf_src = dram_in("b_f", [1, 16])
    st0_src = dram_in("state0", [8, 128, 128])
    ktp_src = dram_in("kT_past", [16, 64, PAST])
    vp_src = dram_in("v_past", [16, 128, 32 * 64])
    lfp_src = dram_in("logf_past", [128, 32 * 16])
    wbf = nc.dram_tensor("wbf", [NWT, 128, 4096], BF16, kind="Internal").ap()
    y_out = dram_out("y", [NTOK, D])
    k_out = dram_out("ko", [NTOK, D])
    v_out = dram_out("vo", [NTOK, D])
    lf_out = dram_out("lfo", [NTOK, 16])
    st_out = dram_out("st", [3, 8, 128, 128])
    out_keys = []
    import os
    DEBUG = int(os.environ.get("KDEBUG", "0"))
    NOINTER = int(os.environ.get("NOINTER", "0"))
    STQ = os.environ.get("STQ", "act")
    CASTENG = os.environ.get("CASTENG", "pool")
    NORMSPLIT = int(os.environ.get("NORMSPLIT", "0"))
    GBANK = int(os.environ.get("GBANK", "1"))
    UBANKS = [int(v) for v in os.environ.get("UBANKS", "3,7").split(",")]
    _zs = [float(v) for v in os.environ.get("ZSCHED", "0,0.6,0,1,0,1").split(",")]
    ZSCHED = [(_zs[0], _zs[1]), (_zs[2], _zs[3]), (_zs[4], _zs[5])]
    KTOKENG = os.environ.get("KTOKENG", "dve")
    dbg_t = {}

    def dbg_dump(name, ap_fn, shape, keys, dt=F32):
        if not DEBUG:
            return
        t = nc.dram_tensor("dbg_" + name, shape, dt, kind="ExternalOutput").ap()
        P.dma("sp", ("dbg", name), lambda e: e.dma_start(out=t, in_=ap_fn()), reads=keys, writes=[("out", len(out_keys))])
        out_keys.append(("out", len(out_keys)))

    def sb(name, shape, dt=F32):
        return es.enter_context(nc.sbuf_tensor(name, shape, dt))

    def psum(name, shape, dt=F32):
        return es.enter_context(nc.psum_tensor(name, shape, dt))

    with es:
        x32 = sb("x32", [128, NCH, TP])
        actA = sb("actA", [128, NCH, TP], BF16)
        actB = sb("actB", [128, NCH, TP], BF16)
        wring = sb("wring", [128, NW, 4096], BF16)
        KT = sb("KT", [128, 8, SEQ], BF16)
        Vc = sb("Vc", [128, 16, 16 * 66], BF16)
        hid = sb("hid", [128, 32 * 256], F32)
        S32 = sb("S32", [128, 2, 8, 128])
        Sbf = sb("Sbf", [128, 8, 8, 128], BF16)
        rstd = sb("rstd", [128, TP])
        identb = sb("identb", [128, 128], BF16)
        ident32 = sb("ident32", [128, 128])
        onesb = sb("onesb", [128, 128], BF16)
        ones32 = sb("ones32", [128, 128])
        TU32 = sb("TU32", [128, 128])
        SU32 = sb("SU32", [128, 128])
        maskBD = sb("maskBD", [128, 128])
        negmask = sb("negmask", [128, 128], BF16)
        Sel = sb("Sel", [128, 16, 128], BF16)
        scanmask = sb("scanmask", [128, TP])
        zerosb = sb("zerosb", [128, 512], BF16)
        vec = sb("vec", [128, 65])
        lbt = sb("lbt", [128, 8])
        omlt = sb("omlt", [128, 8])
        nomlt = sb("nomlt", [128, 8])
        lbtmp = sb("lbtmp", [128, 8])
        bfb = sb("bfb", [128, 16])
        wf32 = sb("wf32", [128, 128])
        wfb = sb("wfb", [128, 8, 16], BF16)
        carry = sb("carry", [128, 16])
        cs = sb("cs", [128, 4, 16])
        nGk = sb("nGk", [128, 16, 16])
        lf32 = sb("lf32", [128, 4, 16])
        pl32 = sb("pl32", [128, 4, 16])
        G32 = sb("G32", [128, 4, 16])
        GT = sb("GT", [128, TP], BF16)
        rl = sb("rl", [128, 2, 4])
        PTr = sb("PTr", [128, 3, TP], BF16)

        PB = {i: psum("pb%d" % i, [128, 512]) for i in range(8)}
        PBb = {i: PB[i][:, :].bitcast(BF16) for i in range(8)}
        bank5 = PBb[5]

        def hv(j0, n, dt=BF16):
            a = hid[:, j0 * 256:(j0 + n) * 256]
            if dt == BF16:
                a = a.bitcast(BF16)
            return a

        def hk(j0, n):
            return [("h", j) for j in range(j0, j0 + n)]

        def cst(eng, fn, writes, reads=()):
            P.op(eng, fn, reads=reads, writes=writes)

        cst("pool", lambda e: e.memset(ident32[:], 0.0), ["ident32"])
        cst("pool", lambda e: e.affine_select(out=ident32[:], in_=ident32[:], pattern=[[-1, 128]],
                                              compare_op=ALU.not_equal, fill=1.0, base=0, channel_multiplier=1),
            ["ident32"], ["ident32"])
        cst("pool", lambda e: e.tensor_copy(out=identb[:], in_=ident32[:]), ["identb"], ["ident32"])
        cst("pool", lambda e: e.memset(ones32[:], 1.0), ["ones32"])
        cst("pool", lambda e: e.memset(onesb[:], 1.0), ["onesb"])
        cst("pool", lambda e: e.memset(zerosb[:], 0.0), ["zerosb"])
        cst("pool", lambda e: e.affine_select(out=TU32[:], in_=ones32[:], pattern=[[1, 128]],
                                              compare_op=ALU.is_ge, fill=0.0, base=0, channel_multiplier=-1),
            ["TU32"], ["ones32"])
        cst("pool", lambda e: e.affine_select(out=SU32[:], in_=ones32[:], pattern=[[-1, 128]],
                                              compare_op=ALU.is_gt, fill=0.0, base=0, channel_multiplier=1),
            ["SU32"], ["ones32"])
        cst("pool", lambda e: e.tensor_copy(out=maskBD[:], in_=TU32[:]), ["maskBD"], ["TU32"])
        cst("pool", lambda e: e.memset(maskBD[0:64, 64:128], 0.0), ["maskBD"], ["maskBD"])
        cst("pool", lambda e: e.tensor_scalar(out=negmask[:], in0=SU32[:], scalar1=NEG, scalar2=None, op0=ALU.mult),
            ["negmask"], ["SU32"])
        cst("pool", lambda e: e.memset(Sel[:], 0.0), ["Sel"])
        cst("pool", lambda e: e.memset(GT[:], 0.0), ["GT"])
        cst("pool", lambda e: e.affine_select(out=Sel[0:16], in_=Sel[0:16], pattern=[[-1, 16], [0, 128]],
                                              compare_op=ALU.not_equal, fill=1.0, base=0, channel_multiplier=1),
            ["Sel"], ["Sel"])
        cst("pool", lambda e: e.memset(scanmask[:], 1.0), ["scanmask"])
        cst("pool", lambda e: e.memset(scanmask[:].rearrange("p (c t) -> p c t", t=64)[:, :, 0:1], 0.0),
            ["scanmask"], ["scanmask"])
        cst("pool", lambda e: e.memset(Vc[:].rearrange("p k (h d) -> p (k h) d", d=66)[:, :, 64:65], 1.0),
            [("V", kb) for kb in range(16)])
        P.dma("sp", "vec", lambda e: e.dma_start(out=vec[:], in_=vec_src), writes=["vec"])
        P.dma("sp", "bfb", lambda e: e.dma_start(out=bfb[:], in_=bf_src.partition_broadcast(128)), writes=["bfb"])
        P.dma("sp", "wf32", lambda e: e.dma_start(out=wf32[:], in_=wf_src), writes=["wf32"])
        cst("dve", lambda e: e.tensor_copy(out=wfb[:].rearrange("p a b -> p (a b)"), in_=wf32[:]), ["wfb"], ["wf32"])
        cst("dve", lambda e: e.tensor_tensor(out=lbtmp[:], in0=vec[:, 48:56], in1=vec[:, 56:64], op=ALU.subtract),
            ["lbtmp"], ["vec"])
        cst("act", lambda e: e.activation(out=lbt[:], in_=lbtmp[:], func=AF.Sigmoid), ["lbt"], ["lbtmp"])
        cst("dve", lambda e: e.tensor_scalar(out=omlt[:], in0=lbt[:], scalar1=-1.0, scalar2=1.0, op0=ALU.mult, op1=ALU.add),
            ["omlt"], ["lbt"])
        cst("dve", lambda e: e.tensor_scalar(out=nomlt[:], in0=lbt[:], scalar1=1.0, scalar2=-1.0, op0=ALU.mult, op1=ALU.add),
            ["nomlt"], ["lbt"])

        epsb = sb("epsb", [128, 1])
        cst("pool", lambda e: e.memset(epsb[:], EPS), ["epsb"])

        for j in range(NWT):
            P.dma("pool", ("wc", j),
                  lambda e, j=j: e.dma_start(out=wbf[j].rearrange("p (a b) -> p a b", b=2048),
                                             in_=wsrc[j].rearrange("p (a b) -> p a b", b=2048)),
                  writes=[("wbf", j)])

        wstate = {"n": 0}

        def wload(j):
            s = wstate["n"] % NW
            wstate["n"] += 1
            P.dma("sp", ("w", s), lambda e: e.dma_start(out=wring[:, s, :], in_=wbf[j]),
                  reads=[("wbf", j)], writes=[("w", s)])
            return wring[:, s, :], ("w", s)

        def bk(b):
            if b == 7:
                return [("pb7", i) for i in range(4)]
            if b == 5:
                return [("pb5", 0), ("pb5", 1)]
            return [("pb", b)]

        def mm_group(bank_keys, out_ap, items):
            n = len(items)
            for i, (l, r, rk) in enumerate(items):
                P.op("pe", lambda e, l=l, r=r, i=i: e.matmul(out_ap, lhsT=l, rhs=r, start=(i == 0), stop=(i == n - 1)),
                     reads=rk, writes=list(bank_keys))

        def norm(T, gcol, dst, dkey, bank):
            for c in range(NCH):
                if NORMSPLIT and c % 2 == 1:
                    P.op("dve", lambda e, c=c: e.tensor_tensor(out=dst[:, c, 0:T], in0=x32[:, c, 0:T], in1=x32[:, c, 0:T], op=ALU.mult),
                         reads=[("x", c)], writes=[(dkey, c)])
                else:
                    P.op("act", lambda e, c=c: e.activation(out=dst[:, c, 0:T], in_=x32[:, c, 0:T], func=AF.Square),
                         reads=[("x", c)], writes=[(dkey, c)])
            bkk = bk(bank)
            mm_group(bkk, PB[bank][:, 0:T], [(onesb[:], dst[:, c, 0:T], ["onesb", (dkey, c)]) for c in range(NCH)])
            P.op("act", lambda e: e.activation(out=rstd[:, 0:T], in_=PB[bank][:, 0:T], func=AF.Ln, bias=epsb[:], scale=1.0 / D),
                 reads=bkk + ["epsb"], writes=["rstd"])
            P.op("act", lambda e: e.activation(out=rstd[:, 0:T], in_=rstd[:, 0:T], func=AF.Exp, scale=-0.5),
                 reads=["rstd"], writes=["rstd"])
            for c in range(NCH):
                P.op("dve", lambda e, c=c: e.scalar_tensor_tensor(out=dst[:, c, 0:T], in0=x32[:, c, 0:T],
                                                                 scalar=vec[:, gcol + c:gcol + c + 1], in1=rstd[:, 0:T],
                                                                 op0=ALU.mult, op1=ALU.mult),
                     reads=[("x", c), "vec", "rstd"], writes=[(dkey, c)])


        def proj_residual(T, wtiles, src, skey, nkc, bank_rot):
            cw = 4096 // nkc
            npc = cw // 128
            c = 0
            for j in wtiles:
                wt, wk = wload(j)
                wv = wt.rearrange("p (k n) -> p k n", n=cw)
                for q in range(npc):
                    b = bank_rot[c % len(bank_rot)]
                    bkk = bk(b)
                    mm_group(bkk, PB[b][:, 0:T],
                             [(wv[:, kc, q * 128:(q + 1) * 128], src(kc)[:, 0:T], [wk, skey(kc)]) for kc in range(nkc)])
                    P.op("dve", lambda e, c=c, b=b: e.tensor_tensor(out=x32[:, c, 0:T], in0=x32[:, c, 0:T], in1=PB[b][:, 0:T], op=ALU.add),
                         reads=[("x", c)] + bkk, writes=[("x", c)])
                    c += 1

        def mlp(T, l, src, skey):
            up0 = 10 if l == 0 else 36
            dn0 = up0 + 8
            hview = hv(0, 32).rearrange("p (c t) -> p c t", t=512)
            n = 0
            for j in range(8):
                wt, wk = wload(up0 + j)
                wv = wt.rearrange("p (k n) -> p k n", n=512)
                for q in range(4):
                    b = n % 4
                    mm_group(bk(b), PB[b][:, 0:T],
                             [(wv[:, kc, q * 128:(q + 1) * 128], src[:, kc, 0:T], [wk, (skey, kc)]) for kc in range(NCH)])
                    P.op("act", lambda e, n=n, b=b: e.activation(out=hview[:, n, 0:T], in_=PB[b][:, 0:T], func=AF.Relu),
                         reads=bk(b), writes=[("h", n)])
                    P.op("pool", lambda e, n=n: e.tensor_tensor(out=hview[:, n, 0:T], in0=hview[:, n, 0:T], in1=hview[:, n, 0:T], op=ALU.mult),
                         reads=[("h", n)], writes=[("h", n)])
                    n += 1
            proj_residual(T, list(range(dn0, dn0 + 8)), lambda kc: hview[:, kc, :], lambda kc: ("h", kc), 32, [4, 6, 7])

        S_par = {}

        def capture(fn):
            saved = P.ops
            P.ops = []
            try:
                fn()
                out = P.ops
            finally:
                P.ops = saved
            return out

        def zipper(LA, LB):
            out = []
            ia = ib = 0
            na, nb = len(LA), len(LB)
            while ia < na or ib < nb:
                if ib >= nb or (ia < na and ia * nb <= ib * na):
                    out.append(LA[ia]); ia += 1
                else:
                    out.append(LB[ib]); ib += 1
            return out

        def zipperN(lists, sched=None):
            idxs = [i for i, L in enumerate(lists) if L]
            if sched is None:
                sched = [(0.0, 1.0)] * len(lists)
            items = []
            for i in idxs:
                L = lists[i]
                off, sc = sched[i]
                for k, o in enumerate(L):
                    items.append((off + sc * k / len(L), i, k, o))
            items.sort(key=lambda t: (t[0], t[1], t[2]))
            return [t[3] for t in items]

        def sigmoid_chain(dst, src_psum, src_keys, dst_keys):
            P.op("act", lambda e: e.activation(out=dst, in_=src_psum, func=AF.Exp, scale=-1.0), reads=src_keys, writes=dst_keys)
            P.op("act", lambda e: e.activation(out=dst, in_=dst, func=AF.Ln, bias=1.0, scale=1.0), reads=dst_keys, writes=dst_keys)
            P.op("act", lambda e: e.activation(out=dst, in_=dst, func=AF.Exp, scale=-1.0), reads=dst_keys, writes=dst_keys)

        def run_hgrn(T, seq_first_tile, sample):
            nblk = max(1, T // 128)
            tb = min(128, T)
            nchk = T // 64
            fbuf = hv(13, 2, F32); kk = hv(15, 1); bc = hv(16, 2, F32); kinv = hv(18, 1); kend = hv(19, 1); scT = hv(20, 1)
            rs2 = hv(29, 2, F32); osq = hv(31, 1); onb = osq
            K = {"f": hk(13, 2), "kk": hk(15, 1), "bc": hk(16, 2), "kinv": hk(18, 1), "kend": hk(19, 1), "scT": hk(20, 1),
                 "rs2": hk(29, 2), "osq": hk(31, 1)}

            def b3(h):
                r = h % 3
                return dict(sq=hv(r, 1), Ksq=hk(r, 1), sg=hv(7 + 2 * r, 1), Ksg=hk(7 + 2 * r, 1),
                            vtok=hv(8 + 2 * r, 1), Kvtok=hk(8 + 2 * r, 1))

            def b2(h):
                p = h % 2
                return dict(sz=hv(3 + 2 * p, 2, F32), Ksz=hk(3 + 2 * p, 2),
                            qdec=hv(21 + 4 * p, 1), Kqdec=hk(21 + 4 * p, 1), ktok=hv(22 + 4 * p, 1), Kktok=hk(22 + 4 * p, 1),
                            eb=hv(23 + 4 * p, 2, F32), Keb=hk(23 + 4 * p, 2))

            P.tag = "hgrn.norm"
            norm(T, 0, actA, "A", 0)

            def stageA1(h):
                X3, X2 = b3(h), b2(h)
                sq, sg, vtok, sz = X3["sq"], X3["sg"], X3["vtok"], X2["sz"]
                P.tag = "hgrn.A.proj"
                wt, wk = wload(h)
                wv = wt.rearrange("p (k n) -> p k n", n=512)
                xs = lambda kc: actA[:, kc, 0:T]
                xk = lambda kc: [wk, ("A", kc)]
                mm_group(bk(0), PB[0][:, 0:T], [(wv[:, kc, 0:128], xs(kc), xk(kc)) for kc in range(NCH)])
                mm_group(bk(1), PB[1][:, 0:T], [(wv[:, kc, 128:256], xs(kc), xk(kc)) for kc in range(NCH)])
                sigmoid_chain(sq[:, 0:T], PB[0][:, 0:T], bk(0), X3["Ksq"])
                P.op("dve", lambda e: e.tensor_tensor(out=sq[:, 0:T], in0=PB[0][:, 0:T], in1=sq[:, 0:T], op=ALU.mult), reads=bk(0) + X3["Ksq"], writes=X3["Ksq"])
                sigmoid_chain(sz[:, 0:T], PB[1][:, 0:T], bk(1), X2["Ksz"])
                mm_group(bk(GBANK), PB[GBANK][:, 0:T], [(wv[:, kc, 384:512], xs(kc), xk(kc)) for kc in range(NCH)])
                sigmoid_chain(sg[:, 0:T], PB[GBANK][:, 0:T], bk(GBANK), X3["Ksg"])
                for bl in range(nblk):
                    mm_group(bk(2), PB[2][0:tb, bl * 128:(bl + 1) * 128],
                             [(actA[:, kc, bl * tb:(bl + 1) * tb], wv[:, kc, 256:384], xk(kc)) for kc in range(NCH)])
                P.op("dve", lambda e: e.tensor_copy(out=vtok[0:tb, 0:nblk * 128], in_=PB[2][0:tb, 0:nblk * 128]),
                     reads=bk(2), writes=X3["Kvtok"])

            def stageA2(h):
                X3, X2 = b3(h), b2(h)
                sq, vtok, sz = X3["sq"], X3["vtok"], X2["sz"]
                qdec, ktok, ebuf = X2["qdec"], X2["ktok"], X2["eb"]
                einv = fbuf
                ob = 5 + (h % 2)
                P.tag = "hgrn.A.sc"
                P.op("dve", lambda e: e.tensor_scalar(out=fbuf[:, 0:T], in0=sz[:, 0:T], scalar1=omlt[:, h:h + 1], scalar2=lbt[:, h:h + 1],
                                                     op0=ALU.mult, op1=ALU.add), reads=X2["Ksz"] + ["omlt", "lbt"], writes=K["f"])
                P.op("dve", lambda e: e.tensor_scalar(out=kk[:, 0:T], in0=sz[:, 0:T], scalar1=nomlt[:, h:h + 1], scalar2=omlt[:, h:h + 1],
                                                     op0=ALU.mult, op1=ALU.add), reads=X2["Ksz"] + ["omlt", "nomlt"], writes=K["kk"])
                P.op("act", lambda e: e.activation(out=fbuf[:, 0:T], in_=fbuf[:, 0:T], func=AF.Ln), reads=K["f"], writes=K["f"])
                P.op("dve", lambda e: e.tensor_tensor_scan(out=bc[:, 0:T], data0=scanmask[:, 0:T], data1=fbuf[:, 0:T], initial=0.0,
                                                           op0=ALU.mult, op1=ALU.add), reads=K["f"] + ["scanmask"], writes=K["bc"])
                P.op("act", lambda e: e.activation(out=ebuf[:, 0:T], in_=bc[:, 0:T], func=AF.Exp), reads=K["bc"], writes=X2["Keb"])
                P.op("act", lambda e: e.activation(out=einv[:, 0:T], in_=bc[:, 0:T], func=AF.Exp, scale=-1.0), reads=K["bc"], writes=K["f"])
                P.op("dve", lambda e: e.tensor_tensor(out=qdec[:, 0:T], in0=sq[:, 0:T], in1=ebuf[:, 0:T], op=ALU.mult),
                     reads=X3["Ksq"] + X2["Keb"], writes=X2["Kqdec"])
                P.op("dve", lambda e: e.tensor_tensor(out=kinv[:, 0:T], in0=kk[:, 0:T], in1=einv[:, 0:T], op=ALU.mult),
                     reads=K["kk"] + K["f"], writes=K["kinv"])
                P.op("pool", lambda e: e.tensor_tensor(
                    out=kend[:, 0:T].rearrange("p (c t) -> p c t", t=64),
                    in0=kinv[:, 0:T].rearrange("p (c t) -> p c t", t=64),
                    in1=ebuf[:, 0:T].rearrange("p (c t) -> p c t", t=64)[:, :, 63:64].to_broadcast([128, nchk, 64]), op=ALU.mult),
                    reads=K["kinv"] + X2["Keb"], writes=K["kend"])
                for bl in range(nblk):
                    P.op("pe", lambda e, bl=bl: e.matmul(PB[4][0:tb, bl * 128:bl * 128 + tb], lhsT=kinv[:, bl * tb:(bl + 1) * tb],
                                                         rhs=qdec[:, bl * tb:(bl + 1) * tb], start=True, stop=True),
                         reads=K["kinv"] + X2["Kqdec"], writes=bk(4))
                if T >= 128:
                    P.op("dve", lambda e: e.tensor_tensor(out=scT[:, 0:T].rearrange("p (b t) -> p b t", t=128),
                                                          in0=PB[4][:, 0:T].rearrange("p (b t) -> p b t", t=128),
                                                          in1=maskBD[:].unsqueeze(1).to_broadcast([128, nblk, 128]), op=ALU.mult),
                         reads=bk(4) + ["maskBD"], writes=K["scT"])
                else:
                    P.op("dve", lambda e: e.tensor_tensor(out=scT[0:tb, 0:tb], in0=PB[4][0:tb, 0:tb], in1=maskBD[0:tb, 0:tb], op=ALU.mult),
                         reads=bk(4) + ["maskBD"], writes=K["scT"])
                for bl in range(nblk):
                    P.op("pe", lambda e, bl=bl: e.transpose(out=PBb[4][0:tb, bl * 128:(bl + 1) * 128], in_=kend[:, bl * tb:(bl + 1) * tb],
                                                            identity=identb[:]),
                         reads=K["kend"] + ["identb"], writes=bk(4))
                if KTOKENG == "act":
                    P.op("act", lambda e: e.copy(out=ktok[0:tb, 0:nblk * 128], in_=PBb[4][0:tb, 0:nblk * 128]), reads=bk(4), writes=X2["Kktok"])
                else:
                    P.op("dve", lambda e: e.tensor_copy(out=ktok[0:tb, 0:nblk * 128], in_=PBb[4][0:tb, 0:nblk * 128]), reads=bk(4), writes=X2["Kktok"])
                P.tag = "hgrn.A.intra"
                for bl in range(nblk):
                    P.op("pe", lambda e, bl=bl: e.matmul(PB[ob][:, bl * tb:(bl + 1) * tb], lhsT=vtok[0:tb, bl * 128:(bl + 1) * 128],
                                                         rhs=scT[0:tb, bl * 128:bl * 128 + tb], start=(bl == 0), stop=False),
                         reads=X3["Kvtok"] + K["scT"], writes=bk(ob))

            def stageB(h):
                X3, X2 = b3(h), b2(h)
                vtok = X3["vtok"]
                qdec, ktok, ebuf = X2["qdec"], X2["ktok"], X2["eb"]
                ob = 5 + (h % 2)
                P.tag = "hgrn.B.chain"
                for c in range(nchk):
                    bl, half = divmod(c, 2)
                    p0 = half * 64
                    ub = UBANKS[c % len(UBANKS)]
                    P.op("pe", lambda e, bl=bl, p0=p0, ub=ub: e.matmul(PB[ub][:, 0:128],
                                                                      lhsT=ktok[p0:p0 + 64, bl * 128:(bl + 1) * 128],
                                                                      rhs=vtok[p0:p0 + 64, bl * 128:(bl + 1) * 128], start=True, stop=True),
                         reads=X2["Kktok"] + X3["Kvtok"], writes=bk(ub))
                    pi = S_par.get(h, 0)
                    po = 1 - pi
                    S_par[h] = po
                    ecol = c * 64 + 63
                    P.op("dve", lambda e, pi=pi, po=po, ub=ub, ecol=ecol: e.scalar_tensor_tensor(
                        out=S32[:, po, h, :], in0=S32[:, pi, h, :], scalar=ebuf[:, ecol:ecol + 1],
                        in1=PB[ub][:, 0:128], op0=ALU.mult, op1=ALU.add),
                        reads=[("S32", pi, h)] + bk(ub) + X2["Keb"], writes=[("S32", po, h)])
                    if c + 1 < nchk:
                        P.op(CASTENG, (lambda e, po=po, c=c: e.tensor_copy(out=Sbf[:, h, c + 1, :], in_=S32[:, po, h, :])) if CASTENG != "act" else
                             (lambda e, po=po, c=c: e.copy(out=Sbf[:, h, c + 1, :], in_=S32[:, po, h, :])),
                             reads=[("S32", po, h)], writes=[("Sbf", h, c + 1)])
                for c in range(nchk):
                    last = (c == nchk - 1)
                    P.op("pe", lambda e, c=c, last=last: e.matmul(PB[ob][:, c * 64:(c + 1) * 64], lhsT=Sbf[:, h, c, :],
                                                                 rhs=qdec[:, c * 64:(c + 1) * 64], start=False, stop=last),
                         reads=[("Sbf", h, c)] + X2["Kqdec"], writes=bk(ob))
                pf = S_par[h]
                P.op(CASTENG, (lambda e, pf=pf: e.tensor_copy(out=Sbf[:, h, 0, :], in_=S32[:, pf, h, :])) if CASTENG != "act" else
                     (lambda e, pf=pf: e.copy(out=Sbf[:, h, 0, :], in_=S32[:, pf, h, :])),
                     reads=[("S32", pf, h)], writes=[("Sbf", h, 0)])

            def stageBn(h):
                X3 = b3(h)
                sg = X3["sg"]
                ob = 5 + (h % 2)
                P.tag = "hgrn.B.norm"
                P.op("act", lambda e: e.activation(out=osq[:, 0:T], in_=PB[ob][:, 0:T], func=AF.Square), reads=bk(ob), writes=K["osq"])
                P.op("pe", lambda e: e.matmul(PB[7][:, 0:T], lhsT=onesb[:], rhs=osq[:, 0:T], start=True, stop=True),
                     reads=K["osq"] + ["onesb"], writes=bk(7))
                P.op("act", lambda e: e.activation(out=rs2[:, 0:T], in_=PB[7][:, 0:T], func=AF.Ln, bias=epsb[:], scale=1.0 / 128),
                     reads=bk(7) + ["epsb"], writes=K["rs2"])
                P.op("act", lambda e: e.activation(out=rs2[:, 0:T], in_=rs2[:, 0:T], func=AF.Exp, scale=-0.5), reads=K["rs2"], writes=K["rs2"])
                P.op("dve", lambda e: e.scalar_tensor_tensor(out=onb[:, 0:T], in0=PB[ob][:, 0:T], scalar=vec[:, 64:65], in1=rs2[:, 0:T],
                                                             op0=ALU.mult, op1=ALU.mult), reads=bk(ob) + ["vec"] + K["rs2"], writes=K["osq"])
                P.op("pool", lambda e: e.tensor_tensor(out=actB[:, h, 0:T], in0=onb[:, 0:T], in1=sg[:, 0:T], op=ALU.mult),
                     reads=K["osq"] + X3["Ksg"], writes=[("B", h)])

            P.ops.extend(capture(lambda: stageA1(0)))
            P.ops.extend(capture(lambda: stageA1(1)))
            P.ops.extend(capture(lambda: stageA2(0)))
            for i in range(8):
                Y = capture(lambda: stageA1(i + 2)) if i + 2 < 8 else []
                X = capture(lambda: stageA2(i + 1)) if i + 1 < 8 else []
                Z = capture(lambda: stageB(i))
                N = capture(lambda: stageBn(i))
                P.ops.extend(zipperN([Y, X, Z], ZSCHED))
                P.ops.extend(N)
            P.tag = "hgrn.wo"
            proj_residual(T, [8, 9], lambda kc: actB[:, kc, :], lambda kc: ("B", kc), 8, [1, 2])

        def run_kv(T, tok0, kt_dst, kt_keys, v_dst, v_keys, nG_dst, nG_key, first):
            nblk = max(1, T // 128)
            tb = min(128, T)
            norm(T, 16, actA, "A", 0)
            stage = [hv(2 * i, 2, F32) for i in range(4)]
            skeys = [hk(2 * i, 2) for i in range(4)]
            si = {"n": 0}
            import os
            kvs = int(os.environ.get("KVSTOP", "9"))
            if kvs < -2:
                return
            wts = [wload(26), wload(27)]
            for pr in range(8):
                wt, wk = wts[pr // 4]
                wv = wt.rearrange("p (k n) -> p k n", n=512)
                b = pr % 2
                mm_group(bk(b), PB[b][:, 0:T],
                         [(wv[:, kc, (pr % 4) * 128:(pr % 4 + 1) * 128], actA[:, kc, 0:T], [wk, ("A", kc)]) for kc in range(NCH)])
                eng = "act" if pr % 2 == 0 else "dve"
                if eng == "act":
                    P.op("act", lambda e, pr=pr, b=b: e.copy(out=kt_dst(pr), in_=PB[b][:, 0:T]), reads=[("pb", b)], writes=kt_keys(pr))
                else:
                    P.op("dve", lambda e, pr=pr, b=b: e.tensor_copy(out=kt_dst(pr), in_=PB[b][:, 0:T]), reads=[("pb", b)], writes=kt_keys(pr))

            if kvs < -1:
                return

            def tokmajor(wtile, dst_dram, extra):
                wt, wk = wtile
                wv = wt.rearrange("p (k n) -> p k n", n=512)
                for bl in range(nblk):
                    b = 2 + (si["n"] % 2)
                    s = si["n"] % 4
                    si["n"] += 1
                    mm_group(bk(b), PB[b][0:tb, :],
                             [(actA[:, kc, bl * tb:(bl + 1) * tb], wv[:, kc, :], [wk, ("A", kc)]) for kc in range(NCH)])
                    P.op("act", lambda e, b=b, s=s: e.copy(out=stage[s][0:tb, :], in_=PB[b][0:tb, :]), reads=[("pb", b)], writes=skeys[s])
                    extra(bl, b)
                    P.dma(STQ, ("st", s), lambda e, s=s, bl=bl: e.dma_start(out=dst_dram(bl), in_=stage[s][0:tb, :]),
                          reads=skeys[s], writes=[("out", len(out_keys))])
                    out_keys.append(("out", len(out_keys)))

            for half in range(2):
                tokmajor(wts[half], lambda bl, half=half: k_out[tok0 + bl * tb: tok0 + (bl + 1) * tb, half * 512:(half + 1) * 512],
                         lambda bl, b: None)
            if kvs < 0:
                return
            vts = [wload(28), wload(29)]
            for half in range(2):
                def vextra(bl, b, half=half):
                    s_ = (si["n"] - 1) % 4
                    P.op("dve", lambda e: e.tensor_copy(out=v_dst(bl)[:, half * 8:(half + 1) * 8, 0:64],
                                                        in_=stage[s_][0:tb, :].rearrange("p (h d) -> p h d", d=64)),
                         reads=skeys[s_], writes=v_keys(bl))
                tokmajor(vts[half], lambda bl, half=half: v_out[tok0 + bl * tb: tok0 + (bl + 1) * tb, half * 512:(half + 1) * 512], vextra)
            if kvs < 1:
                return
            for bl in range(nblk):
                mm_group(bk(4), PB[4][0:tb, bl * 16:(bl + 1) * 16],
                         [(actA[:, kc, bl * tb:(bl + 1) * tb], wfb[:, kc, :], ["wfb", ("A", kc)]) for kc in range(NCH)])
            nb16 = nblk * 16
            plv = pl32[:].rearrange("p a b -> p (a b)")
            lfv = lf32[:].rearrange("p a b -> p (a b)")
            P.op("dve", lambda e: e.tensor_tensor(out=pl32[0:tb, 0:nblk, :], in0=PB[4][0:tb, 0:nb16].rearrange("p (a b) -> p a b", b=16),
                                                  in1=bfb[0:tb, :].unsqueeze(1).to_broadcast([tb, nblk, 16]), op=ALU.add),
                 reads=[("pb", 4), "bfb"], writes=["pl32"])
            P.op("act", lambda e: e.activation(out=plv[0:tb, 0:nb16], in_=plv[0:tb, 0:nb16], func=AF.Exp, scale=-1.0), reads=["pl32"], writes=["pl32"])
            P.op("act", lambda e: e.activation(out=plv[0:tb, 0:nb16], in_=plv[0:tb, 0:nb16], func=AF.Ln, bias=1.0, scale=1.0), reads=["pl32"], writes=["pl32"])
            P.op("dve", lambda e: e.tensor_scalar(out=lfv[0:tb, 0:nb16], in0=plv[0:tb, 0:nb16], scalar1=-1.0, scalar2=None, op0=ALU.mult),
                 reads=["pl32"], writes=["lf32"])
            P.dma(STQ, "lfo", lambda e: e.dma_start(out=lf_out[tok0:tok0 + T, :].rearrange("(b p) h -> p b h", p=tb), in_=lf32[0:tb, 0:nblk, :]),
                  reads=["lf32"], writes=[("out", len(out_keys))])
            out_keys.append(("out", len(out_keys)))
            if kvs < 2:
                return
            if first:
                P.op("dve", lambda e: e.memset(carry[:], 0.0), writes=["carry"])
            P.op("pe", lambda e: e.matmul(PB[6][0:tb, 0:nb16], lhsT=TU32[0:tb, 0:tb], rhs=lfv[0:tb, 0:nb16], start=True, stop=True),
                 reads=["TU32", "lf32"], writes=[("pb", 6)])
            P.op("pe", lambda e: e.matmul(PB[7][0:tb, 0:nb16], lhsT=ones32[0:tb, 0:tb], rhs=lfv[0:tb, 0:nb16], start=True, stop=True),
                 reads=["ones32", "lf32"], writes=[("pb7", 0), ("pb7", 1), ("pb7", 2), ("pb7", 3)])
            P.op("dve", lambda e: e.tensor_copy(out=cs[0:tb, 0, :], in_=carry[0:tb, :]), reads=["carry"], writes=["cs"])
            for bl in range(nblk):
                dst = cs[0:tb, bl + 1, :] if bl + 1 < nblk else carry[0:tb, :]
                P.op("dve", lambda e, bl=bl, dst=dst: e.tensor_tensor(out=dst, in0=cs[0:tb, bl, :], in1=PB[7][0:tb, bl * 16:(bl + 1) * 16], op=ALU.add),
                     reads=["cs", ("pb7", 0)], writes=(["cs"] if bl + 1 < nblk else ["carry"]))
            P.op("dve", lambda e: e.tensor_tensor(out=G32[0:tb, 0:nblk, :], in0=PB[6][0:tb, 0:nb16].rearrange("p (a b) -> p a b", b=16),
                                                  in1=cs[0:tb, 0:nblk, :], op=ALU.add), reads=[("pb", 6), "cs"], writes=["G32"])
            for bl in range(nblk):
                P.op("dve", lambda e, bl=bl: e.tensor_scalar(out=nG_dst(bl), in0=G32[0:tb, bl, :], scalar1=-1.0, scalar2=None, op0=ALU.mult),
                     reads=["G32"], writes=[nG_key])
            if kvs < 3:
                return
            for bl in range(nblk):
                P.op("pe", lambda e, bl=bl: e.transpose(out=PB[4][0:16, bl * tb:(bl + 1) * tb], in_=G32[0:tb, bl, :], identity=ident32[0:tb, 0:tb]),
                     reads=["G32", "ident32"], writes=[("pb", 4)])
            P.op("dve", lambda e: e.tensor_copy(out=GT[0:16, 0:T], in_=PB[4][0:16, 0:T]), reads=[("pb", 4), "GT"], writes=["GT"])

        def run_attn(T, keyblocks_for_head, qtile_i):
            nblk = max(1, T // 128)
            tb = min(128, T)
            norm(T, 24, actB, "B", 0)
            QTz = hv(0, 16).rearrange("p (h t) -> p h t", t=512)
            qk = lambda h: [("h", h)]
            sgt = hv(16, 8).rearrange("p (b f) -> p b f", f=1024)
            sgk = lambda bl: hk(16 + 2 * bl, 2)
            PT = [PTr[:, i, :] for i in range(3)]
            PTk = [[("PT", i)] for i in range(3)]
            QTp = hv(0, 16).rearrange("p (c two t) -> p c two t", two=2, t=512)
            P.op("pool", lambda e: e.memset(QTp[64:128, :, 0, :], 0.0), writes=[("h", 2 * c) for c in range(8)])
            P.op("pool", lambda e: e.memset(QTp[0:64, :, 1, :], 0.0), writes=[("h", 2 * c + 1) for c in range(8)])
            ats = int(os.environ.get("ATTSTOP", "99"))
            if ats < 1:
                return
            wq = [wload(30), wload(31)]
            for pr in range(8):
                wt, wk = wq[pr // 4]
                wv = wt.rearrange("p (k n) -> p k n", n=512)
                b = pr % 2
                mm_group(bk(b), PB[b][:, 0:T],
                         [(wv[:, kc, (pr % 4) * 128:(pr % 4 + 1) * 128], actB[:, kc, 0:T], [wk, ("B", kc)]) for kc in range(NCH)])
                P.op("dve", lambda e, pr=pr, b=b: e.tensor_scalar(out=QTz[0:64, 2 * pr, 0:T], in0=PB[b][0:64, 0:T], scalar1=0.125, scalar2=None, op0=ALU.mult),
                     reads=bk(b), writes=qk(2 * pr))
                P.op("dve", lambda e, pr=pr, b=b: e.tensor_scalar(out=QTz[64:128, 2 * pr + 1, 0:T], in0=PB[b][64:128, 0:T], scalar1=0.125, scalar2=None, op0=ALU.mult),
                     reads=bk(b), writes=qk(2 * pr + 1))
            if ats < 2:
                return
            wg = [wload(32), wload(33)]
            n = 0
            for half in range(2):
                wt, wk = wg[half]
                wv = wt.rearrange("p (k n) -> p k n", n=512)
                for bl in range(nblk):
                    b = 2 + n % 2
                    n += 1
                    mm_group(bk(b), PB[b][0:tb, :],
                             [(actB[:, kc, bl * tb:(bl + 1) * tb], wv[:, kc, :], [wk, ("B", kc)]) for kc in range(NCH)])
                    sigmoid_chain(sgt[0:tb, bl, half * 512:(half + 1) * 512], PB[b][0:tb, :], bk(b), sgk(bl))
            if ats < 3:
                return
            nheads = int(os.environ.get("ATTHEADS", "16"))
            attpv = int(os.environ.get("ATTPV", "1"))
            attsel = int(os.environ.get("ATTSEL", "1"))
            SKEW = 2
            items = []
            nxt_head = {"h": 0}

            def expand():
                h = nxt_head["h"]
                nxt_head["h"] += 1
                kbs = keyblocks_for_head(h)
                for j, kb in enumerate(kbs):
                    items.append((h, kb, j == 0, j == len(kbs) - 1))

            def emit_S(i):
                h, kb, first, last = items[i]
                nk = kb["nk"]
                dj = kb["diag"]
                c0 = 0 if dj is None else dj * tb
                N = T - c0
                sbank = i % 5
                pti = i % 3
                sk = bk(sbank)
                sc = PB[sbank]
                P.op("pe", lambda e: e.matmul(sc[0:nk, 0:N], lhsT=kb["kt"], rhs=QTz[:, h, c0:T], start=True, stop=False),
                     reads=kb["kt_keys"] + qk(h), writes=sk)
                P.op("pe", lambda e: e.matmul(sc[0:nk, 0:N], lhsT=Sel[:, h, 0:nk], rhs=GT[:, c0:T], start=False, stop=(dj is None)),
                     reads=["Sel", "GT"], writes=sk)
                if dj is not None:
                    P.op("pe", lambda e: e.matmul(sc[0:nk, 0:nk], lhsT=identb[:, 0:nk], rhs=negmask[:, 0:nk], start=False, stop=True),
                         reads=["identb", "negmask"], writes=sk)
                P.op("act", lambda e: e.activation(out=PT[pti][0:nk, 0:N], in_=sc[0:nk, 0:N], func=AF.Exp, bias=kb["bias"], scale=1.0),
                     reads=sk + kb["bias_keys"], writes=PTk[pti])

            def emit_PV(i):
                h, kb, first, last = items[i]
                nk = kb["nk"]
                dj = kb["diag"]
                c0 = 0 if dj is None else dj * tb
                pti = i % 3
                ab = 6 + (h % 2)
                abk = bk(ab)
                acc = PB[ab][:, 0:nblk * 65].rearrange("p (b d) -> p b d", d=65)
                if first:
                    P.op("pe", lambda e: e.matmul(PB[ab][0:tb, 0:nblk * 65], lhsT=zerosb[:, 0:tb], rhs=zerosb[:, 0:nblk * 65], start=True, stop=False),
                         reads=["zerosb"], writes=abk)
                for bl in range(nblk):
                    if bl * tb < c0:
                        continue
                    lastmm = last and bl == nblk - 1
                    P.op("pe", lambda e, bl=bl, lastmm=lastmm: e.matmul(
                        acc[0:tb, bl, :], lhsT=PT[pti][0:nk, bl * tb - c0:bl * tb - c0 + tb], rhs=kb["v"], start=False, stop=lastmm),
                        reads=PTk[pti] + kb["v_keys"], writes=abk)
                if last:
                    ri = h % 2
                    P.op("dve", lambda e: e.reciprocal(out=rl[0:tb, ri, 0:nblk], in_=acc[0:tb, :, 64]), reads=abk, writes=[("rl", ri)])
                    for bl in range(nblk):
                        P.op("dve", lambda e, bl=bl: e.scalar_tensor_tensor(
                            out=sgt[0:tb, bl, h * 64:(h + 1) * 64], in0=acc[0:tb, bl, 0:64], scalar=rl[0:tb, ri, bl:bl + 1],
                            in1=sgt[0:tb, bl, h * 64:(h + 1) * 64], op0=ALU.mult, op1=ALU.mult),
                            reads=abk + [("rl", ri)] + sgk(bl), writes=sgk(bl))

            i_s = 0
            idx = 0
            expand()
            while idx < len(items):
                while i_s <= idx + SKEW:
                    if i_s >= len(items):
                        if nxt_head["h"] < 16:
                            expand()
                        else:
                            break
                    emit_S(i_s)
                    i_s += 1
                emit_PV(idx)
                idx += 1
            if ats < 4:
                return
            for c in range(NCH):
                tbk = 2 + (c % 2)
                for bl in range(nblk):
                    P.op("pe", lambda e, c=c, bl=bl, tbk=tbk: e.transpose(out=PBb[tbk][:, bl * tb:(bl + 1) * tb],
                                                                       in_=sgt[0:tb, bl, c * 128:(c + 1) * 128], identity=identb[0:tb, 0:tb]),
                         reads=sgk(bl) + ["identb"], writes=bk(tbk))
                P.op("dve", lambda e, c=c, tbk=tbk: e.tensor_copy(out=actA[:, c, 0:T], in_=PBb[tbk][:, 0:T]), reads=bk(tbk), writes=[("A", c)])
            if ats < 5:
                return
            proj_residual(T, [34, 35], lambda kc: actA[:, kc, :], lambda kc: ("A", kc), 8, [0, 1])

        def run_final(T, tok0):
            nblk = max(1, T // 128)
            tb = min(128, T)
            ys = hv(0, 16, F32).rearrange("p (b f) -> p b f", f=1024)
            ysk = hk(0, 16)
            ytmp = [hv(16, 2, F32), hv(18, 2, F32)]
            ytk = [hk(16, 2), hk(18, 2)]
            for c in range(NCH):
                P.op("act", lambda e, c=c: e.activation(out=actA[:, c, 0:T], in_=x32[:, c, 0:T], func=AF.Square), reads=[("x", c)], writes=[("A", c)])
            mm_group(bk(0), PB[0][:, 0:T], [(onesb[:], actA[:, c, 0:T], ["onesb", ("A", c)]) for c in range(NCH)])
            P.op("act", lambda e: e.activation(out=rstd[:, 0:T], in_=PB[0][:, 0:T], func=AF.Ln, bias=epsb[:], scale=1.0 / D), reads=[("pb", 0), "epsb"], writes=["rstd"])
            P.op("act", lambda e: e.activation(out=rstd[:, 0:T], in_=rstd[:, 0:T], func=AF.Exp, scale=-0.5), reads=["rstd"], writes=["rstd"])
            for c in range(NCH):
                t = c % 2
                b = 1 + (c % 2)
                P.op("dve", lambda e, c=c, t=t: e.scalar_tensor_tensor(out=ytmp[t][:, 0:T], in0=x32[:, c, 0:T], scalar=vec[:, 40 + c:41 + c], in1=rstd[:, 0:T],
                                                                      op0=ALU.mult, op1=ALU.mult), reads=[("x", c), "vec", "rstd"], writes=ytk[t])
                for bl in range(nblk):
                    P.op("pe", lambda e, t=t, b=b, bl=bl: e.transpose(out=PB[b][0:tb, bl * 128:(bl + 1) * 128], in_=ytmp[t][:, bl * tb:(bl + 1) * tb], identity=ident32[:]),
                         reads=ytk[t] + ["ident32"], writes=[("pb", b)])
                P.op("act", lambda e, c=c, b=b: e.copy(out=ys[0:tb, 0:nblk, c * 128:(c + 1) * 128], in_=PB[b][0:tb, 0:nblk * 128].rearrange("p (b f) -> p b f", f=128)),
                     reads=[("pb", b)], writes=ysk)
            P.dma(STQ, "yo", lambda e: e.dma_start(out=y_out[tok0:tok0 + T, :].rearrange("(b p) f -> p b f", p=tb), in_=ys[0:tb, 0:nblk, :]),
                  reads=ysk, writes=[("out", len(out_keys))])
            out_keys.append(("out", len(out_keys)))

        tiles = []
        for s in range(NPS):
            for i in range(SEQ // TP):
                tiles.append((s, i, TP, s * SEQ + i * TP))
        tiles.append((NPS, 0, TS, NPS * SEQ))
        if tile_limit is not None:
            tiles = tiles[:tile_limit]

        KTv = KT[:].rearrange("p c (i t) -> p c i t", t=TP)
        Vcv = Vc[:].rearrange("p k (h d) -> p k h d", d=66)
        KTp = KT[:].rearrange("p (s c) t -> p s (c t)", c=2)
        KTn = sb("KTn", [128, 8, TS], BF16)
        Vn = sb("Vn", [TS, 16, 66], BF16)
        Vpv = Vc[:].rearrange("p (s k) f -> p s (k f)", k=2)

        for (s, i, T, tok0) in tiles:
            sample = (s == NPS)
            for c in range(NCH):
                P.dma("sp", ("x", c), lambda e, T=T, tok0=tok0, c=c: e.dma_start(out=x32[:, c, 0:T], in_=xT[c * 128:(c + 1) * 128, tok0:tok0 + T]),
                      writes=[("x", c)])
            if i == 0:
                for h in range(8):
                    S_par[h] = 0
                if sample:
                    P.dma("sp", "st0", lambda e: e.dma_start(out=S32[:, 0, :, :], in_=st0_src.rearrange("h k v -> k h v")),
                          writes=[("S32", 0, h) for h in range(8)])
                    for h in range(8):
                        P.op("act", lambda e, h=h: e.copy(out=Sbf[:, h, 0, :], in_=S32[:, 0, h, :]), reads=[("S32", 0, h)], writes=[("Sbf", h, 0)])
                else:
                    P.op("pool", lambda e: e.memset(S32[:, 0, :, :], 0.0), writes=[("S32", 0, h) for h in range(8)])
                    P.op("pool", lambda e: e.memset(Sbf[:, :, 0, :], 0.0), writes=[("Sbf", h, 0) for h in range(8)])
            if phase_limit < 1:
                continue
            P.tag = "hgrn"
            run_hgrn(T, i == 0, sample)
            if i == 0 and s == 0:
                dbg_dump("x_hgrn", lambda: x32[:].rearrange("p c t -> p (c t)"), [128, 4096], [("x", c) for c in range(NCH)])
            if (not sample and i == SEQ // TP - 1) or sample:
                po = S_par[0]
                P.dma(STQ, "sto", lambda e, s=s, po=po: e.dma_start(out=st_out[s].rearrange("h k v -> k h v"), in_=S32[:, po, :, :]),
                      reads=[("S32", po, h) for h in range(8)], writes=[("out", len(out_keys))])
                out_keys.append(("out", len(out_keys)))
            if phase_limit < 2:
                continue
            P.tag = "mlp0"
            norm(T, 8, actA, "A", 0)
            mlp(T, 0, actA, "A")
            if i == 0 and s == 0:
                dbg_dump("x_mlp0", lambda: x32[:].rearrange("p c t -> p (c t)"), [128, 4096], [("x", c) for c in range(NCH)])
            if phase_limit < 3:
                continue
            P.tag = "kv"
            if not sample:
                run_kv(T, tok0,
                       lambda pr, i=i: KTv[:, pr, i, :], lambda pr, i=i: [("KT", pr, i)],
                       lambda bl, i=i: Vcv[:, 4 * i + bl, :, :], lambda bl, i=i: [("V", 4 * i + bl)],
                       lambda bl, i=i: nGk[:, 4 * i + bl, :], "nGk", i == 0)

                def kbs_prompt(h, i=i):
                    pr, hh = divmod(h, 2)
                    base = hh * 64
                    out = []
                    for kb in range(4 * i + 4):
                        ti, kl = divmod(kb, 4)
                        out.append(dict(nk=128, diag=(kb - 4 * i if kb >= 4 * i else None),
                                        kt=KTv[:, pr, ti, kl * 128:(kl + 1) * 128], kt_keys=[("KT", pr, ti)],
                                        v=Vcv[:, kb, h, 0:65], v_keys=[("V", kb)],
                                        bias=nGk[:, kb, h:h + 1], bias_keys=["nGk"]))
                    return out
                if phase_limit >= 4:
                    P.tag = "attn"
                    run_attn(T, kbs_prompt, i)
            else:
                nGp = hv(30, 2, F32).rearrange("p (a b) -> p a b", b=16)
                lfp = hv(24, 2, F32).rearrange("p (a b) -> p a b", b=16)
                sfxA = hv(26, 2, F32).rearrange("p (a b) -> p a b", b=16)
                sfxB = hv(28, 2, F32).rearrange("p (a b) -> p a b", b=16)
                P.dma("sp", "lfp", lambda e: e.dma_start(out=lfp[:].rearrange("p a b -> p (a b)"), in_=lfp_src), writes=hk(24, 2))
                lfpv = lfp[:].rearrange("p a b -> p (a b)")
                P.op("pe", lambda e: e.matmul(PB[2][:, :], lhsT=SU32[:], rhs=lfpv, start=True, stop=True), reads=["SU32"] + hk(24, 2), writes=[("pb", 2)])
                P.op("pe", lambda e: e.matmul(PB[3][:, :], lhsT=ones32[:], rhs=lfpv, start=True, stop=True), reads=["ones32"] + hk(24, 2), writes=[("pb", 3)])
                P.op("dve", lambda e: e.tensor_copy(out=sfxA[:].rearrange("p a b -> p (a b)"), in_=PB[3][:, :]), reads=[("pb", 3)], writes=hk(26, 2))
                cur, nxt, ck, nk_ = sfxA, sfxB, hk(26, 2), hk(28, 2)
                for sh in (1, 2, 4, 8, 16):
                    P.op("dve", lambda e, cur=cur, nxt=nxt, sh=sh: e.tensor_tensor(out=nxt[:, 0:32 - sh, :], in0=cur[:, 0:32 - sh, :], in1=cur[:, sh:32, :], op=ALU.add),
                         reads=ck, writes=nk_)
                    P.op("dve", lambda e, cur=cur, nxt=nxt, sh=sh: e.tensor_copy(out=nxt[:, 32 - sh:32, :], in_=cur[:, 32 - sh:32, :]), reads=ck, writes=nk_)
                    cur, nxt, ck, nk_ = nxt, cur, nk_, ck
                P.op("dve", lambda e, cur=cur: e.tensor_tensor(out=nGp[:, 0:31, :], in0=PB[2][:, 0:31 * 16].rearrange("p (a b) -> p a b", b=16), in1=cur[:, 1:32, :], op=ALU.add),
                     reads=[("pb", 2)] + ck, writes=hk(30, 2))
                P.op("dve", lambda e: e.tensor_copy(out=nGp[:, 31, :], in_=PB[2][:, 31 * 16:32 * 16]), reads=[("pb", 2)], writes=hk(30, 2))
                P.op("pool", lambda e: e.memset(Vpv[:].rearrange("p s (k d) -> p (s k) d", d=66)[:, :, 64:65], 1.0),
                     reads=[("V", kb) for kb in range(16)], writes=[("V", kb) for kb in range(16)])
                P.op("pool", lambda e: e.memset(Vn[:, :, 64:65], 1.0), writes=["Vn"])
                run_kv(T, tok0,
                       lambda pr: KTn[:, pr, :], lambda pr: [("KTn", pr)],
                       lambda bl: Vn[:, :, :], lambda bl: ["Vn"],
                       lambda bl: nGk[0:TS, 0, :], "nGk", True)

                def kbs_sample(h):
                    pr, hh = divmod(h, 2)
                    base = hh * 64
                    ks = h % 2
                    vs = h % 8
                    ktk = [("KT", 2 * ks + cc, ii) for cc in range(2) for ii in range(4)]
                    vk = [("V", 2 * vs), ("V", 2 * vs + 1)]
                    P.dma("pool", ("ktp", ks), lambda e: e.dma_start(out=KTp[base:base + 64, ks, :].rearrange("p (a b) -> p a b", b=2048),
                                                                     in_=ktp_src[h].rearrange("p (a b) -> p a b", b=2048)), writes=ktk)
                    P.dma("pool", ("vp", vs), lambda e: e.dma_start(out=Vpv[:, vs, :].rearrange("p (k d) -> p k d", d=66)[:, :, 0:64],
                                                                    in_=vp_src[h].rearrange("p (k d) -> p k d", d=64)), writes=vk)
                    out = []
                    for kb in range(PAST // 128):
                        out.append(dict(nk=128, diag=None,
                                        kt=KTp[:, ks, kb * 128:(kb + 1) * 128], kt_keys=ktk,
                                        v=Vpv[:, vs, kb * 66:kb * 66 + 65], v_keys=vk,
                                        bias=nGp[:, kb, h:h + 1], bias_keys=hk(30, 2)))
                    out.append(dict(nk=TS, diag=0, kt=KTn[:, pr, :], kt_keys=[("KTn", pr)],
                                    v=Vn[:, h, 0:65], v_keys=["Vn"], bias=nGk[0:TS, 0, h:h + 1], bias_keys=["nGk"]))
                    return out
                if phase_limit >= 4:
                    P.tag = "attn"
                    run_attn(T, kbs_sample, 0)
            if phase_limit < 5:
                continue
            P.tag = "mlp1"
            norm(T, 32, actB, "B", 0)
            mlp(T, 1, actB, "B")
            P.tag = "final"
            run_final(T, tok0)

        P.op("sp", lambda e: e.nop(), reads=list(out_keys))
        P.analyze()
        lanes = P.lanes()
        sems = {l: es.enter_context(nc.semaphore("s%d" % i)) for i, l in enumerate(lanes)}
        print("[kernel] ops=%d lanes=%d waits=%d" % (len(P.ops), len(lanes), P.n_waits), flush=True)
        with nc.Block() as block:
            P.emit(sems, block)
    _LAST["P"] = P
    return nc


def _tile_w(W, cw):
    K, N = W.shape
    kc = K // 128
    out = []
    for j in range(N // cw):
        blk = W[:, j * cw:(j + 1) * cw].reshape(kc, 128, cw).transpose(1, 0, 2).reshape(128, kc * cw)
        out.append(blk)
    return out


def _pack_weights(w_in_a, w_o_a, w_up, w_down, w_kv, w_q_b, w_o_b):
    tiles = []
    wi = w_in_a[0]
    for h in range(8):
        cols = np.concatenate([wi[:, g * 1024 + h * 128: g * 1024 + (h + 1) * 128] for g in range(4)], axis=1)
        tiles += _tile_w(cols, 512)
    tiles += _tile_w(w_o_a[0], 512)
    tiles += _tile_w(w_up[0], 512)
    tiles += _tile_w(w_down[0], 128)
    tiles += _tile_w(w_kv[:, 0:2048], 512)
    tiles += _tile_w(w_q_b[0], 512)
    tiles += _tile_w(w_o_b[0], 512)
    tiles += _tile_w(w_up[1], 512)
    tiles += _tile_w(w_down[1], 128)
    assert len(tiles) == NWT
    return np.ascontiguousarray(np.stack(tiles, axis=0).astype(np.float32))


def _prepare(x_prompt, x_sample, state_hgrn, cache_k, cache_v, cache_logf, norm_a, w_in_a, lb_logits, g_norm_a, w_o_a,
             norm_kv, w_kv, b_f, norm_b, w_q_b, w_o_b, norm_mlp, w_up, w_down, norm_f, cores=range(8)):
    f32 = np.float32
    wsrc = _pack_weights(w_in_a, w_o_a, w_up, w_down, w_kv, w_q_b, w_o_b)
    wf = np.ascontiguousarray(w_kv[:, 2048:2064].reshape(8, 128, 16).transpose(1, 0, 2).reshape(128, 128))

    def fm(v):
        return v.reshape(8, 128).T

    vecs = np.concatenate([fm(norm_a[0]), fm(norm_mlp[0]), fm(norm_kv), fm(norm_b[0]), fm(norm_mlp[1]), fm(norm_f),
                           fm(lb_logits[0]), fm(lb_logits[1]), g_norm_a[0].reshape(128, 1)], axis=1)
    vecs = np.ascontiguousarray(vecs.astype(f32))
    in_maps = []
    for c in cores:
        xt = np.concatenate([x_prompt[2 * c].T, x_prompt[2 * c + 1].T, x_sample[c].T], axis=1)
        kT = cache_k[c].transpose(1, 2, 0)
        vp = cache_v[c].reshape(32, 128, 16, 64).transpose(2, 1, 0, 3).reshape(16, 128, 32 * 64)
        lp = cache_logf[c].reshape(32, 128, 16).transpose(1, 0, 2).reshape(128, 32 * 16)
        in_maps.append({
            "xT": np.ascontiguousarray(xt), "wsrc": wsrc, "wf": wf, "vecs": vecs,
            "b_f": np.ascontiguousarray(b_f.reshape(1, 16)),
            "state0": np.ascontiguousarray(state_hgrn[c, 0]),
            "kT_past": np.ascontiguousarray(kT), "v_past": np.ascontiguousarray(vp),
            "logf_past": np.ascontiguousarray(lp),
        })
    return in_maps


_NC_CACHE = {}
_LAST = {}


def kernel(x_prompt, x_sample, state_hgrn, cache_k, cache_v, cache_logf,
           norm_a, w_in_a, lb_logits, g_norm_a, w_o_a,
           norm_kv, w_kv, b_f, norm_b, w_q_b, w_o_b,
           norm_mlp, w_up, w_down, norm_f):
    f32 = np.float32
    args = [np.asarray(a, dtype=f32) for a in (x_prompt, x_sample, state_hgrn, cache_k, cache_v, cache_logf,
                                               norm_a, w_in_a, lb_logits, g_norm_a, w_o_a, norm_kv, w_kv, b_f,
                                               norm_b, w_q_b, w_o_b, norm_mlp, w_up, w_down, norm_f)]
    (x_prompt, x_sample, state_hgrn, cache_k, cache_v, cache_logf, norm_a, w_in_a, lb_logits, g_norm_a, w_o_a,
     norm_kv, w_kv, b_f, norm_b, w_q_b, w_o_b, norm_mlp, w_up, w_down, norm_f) = args
    in_maps = _prepare(x_prompt, x_sample, state_hgrn, cache_k, cache_v, cache_logf, norm_a, w_in_a, lb_logits, g_norm_a, w_o_a,
                       norm_kv, w_kv, b_f, norm_b, w_q_b, w_o_b, norm_mlp, w_up, w_down, norm_f)
    if "nc" not in _NC_CACHE:
        _NC_CACHE["nc"] = build()
    nc = _NC_CACHE["nc"]
    res = run_bass_kernel_spmd(nc, in_maps, core_ids=list(range(8)))
    R = res.results
    B, Bd = 16, 8
    y_p = np.empty((B, SEQ, D), f32); k_p = np.empty((B, SEQ, 16, 64), f32); v_p = np.empty((B, SEQ, 16, 64), f32)
    lf_p = np.empty((B, SEQ, 16), f32); st_p = np.empty((B, 1, 8, 128, 128), f32)
    y_s = np.empty((Bd, TS, D), f32); k_s = np.empty((Bd, TS, 16, 64), f32); v_s = np.empty((Bd, TS, 16, 64), f32)
    lf_s = np.empty((Bd, TS, 16), f32); st_s = np.empty((Bd, 1, 8, 128, 128), f32)
    for c in range(8):
        r = R[c]
        for s in range(2):
            b = 2 * c + s
            sl = slice(s * SEQ, (s + 1) * SEQ)
            y_p[b] = r["y"][sl]
            k_p[b] = r["ko"][sl].reshape(SEQ, 16, 64)
            v_p[b] = r["vo"][sl].reshape(SEQ, 16, 64)
            lf_p[b] = r["lfo"][sl]
            st_p[b, 0] = r["st"][s]
        sl = slice(2 * SEQ, 2 * SEQ + TS)
        y_s[c] = r["y"][sl]
        k_s[c] = r["ko"][sl].reshape(TS, 16, 64)
        v_s[c] = r["vo"][sl].reshape(TS, 16, 64)
        lf_s[c] = r["lfo"][sl]
        st_s[c, 0] = r["st"][2]
    return (y_p, y_s, st_p, k_p, v_p, lf_p, st_s, k_s, v_s, lf_s)
```

```python
import numpy as np
from contextlib import ExitStack
import concourse.bass as bass
import concourse.mybir as mybir
from concourse.bass_utils import run_bass_kernel_spmd

F32 = mybir.dt.float32
BF16 = mybir.dt.bfloat16
ALU = mybir.AluOpType
AF = mybir.ActivationFunctionType

COMPUTE = ("pe", "act", "dve", "pool")
QUEUES = ("pe", "act", "dve", "pool", "sp")

D = 1024
NCH = 8
TP = 512
SEQ = 2048
NPS = 2
TS = 64
PAST = 4096
NTOK = NPS * SEQ + TS
EPS = 1e-6
NWT = 52
NW = 4
NEG = -30000.0


class Op:
    __slots__ = ("eng", "fn", "reads", "writes", "lane", "inc", "deps", "sig", "val", "waits", "idx", "tag")

    def __init__(self, eng, fn, reads, writes, lane):
        self.eng = eng
        self.fn = fn
        self.reads = reads
        self.writes = writes
        self.lane = lane
        self.inc = 1 if lane in COMPUTE else 16
        self.deps = []
        self.sig = False
        self.val = None
        self.waits = []


class Prog:
    def __init__(self):
        self.ops = []
        self.tag = ""

    def op(self, eng, fn, reads=(), writes=()):
        o = Op(eng, fn, tuple(reads), tuple(writes), eng)
        o.tag = self.tag
        self.ops.append(o)
        return o

    def dma(self, queue, lane, fn, reads=(), writes=()):
        o = Op(queue, fn, tuple(reads), tuple(writes), ("dma", lane))
        o.tag = self.tag
        self.ops.append(o)
        return o

    def analyze(self):
        last_w = {}
        readers = {}
        last_on_lane = {}
        for i, o in enumerate(self.ops):
            o.idx = i
            deps = {}

            def add(d):
                if d is o:
                    return
                if d.lane == "pe" and o.lane == "pe":
                    return
                deps[d.idx] = d

            for k in o.reads:
                w = last_w.get(k)
                if w is not None:
                    add(w)
            for k in o.writes:
                w = last_w.get(k)
                if w is not None and not (w.lane == o.lane and o.lane in COMPUTE):
                    add(w)
                for r in readers.get(k, {}).values():
                    if not (r.lane == o.lane and o.lane in COMPUTE):
                        add(r)
            if o.lane not in COMPUTE:
                p = last_on_lane.get(o.lane)
                if p is not None:
                    add(p)
                last_on_lane[o.lane] = o
            for k in o.reads:
                readers.setdefault(k, {})[o.lane] = o
            for k in o.writes:
                last_w[k] = o
                readers[k] = {}
            o.deps = list(deps.values())
        ordc = {}
        for o in self.ops:
            ordc[o.lane] = ordc.get(o.lane, 0) + 1
            o.val = ordc[o.lane]
            o.sig = o.lane not in COMPUTE
        known = {q: {} for q in QUEUES}
        clocks = {}
        wait_ops = {}
        nw = 0
        for o in self.ops:
            kn = known[o.eng]
            wl = []
            for d in sorted(o.deps, key=lambda d: -d.idx):
                if kn.get(d.lane, 0) >= d.val:
                    continue
                wl.append(d)
                d.sig = True
                nw += 1
                for l, v in clocks[d.idx].items():
                    if kn.get(l, 0) < v:
                        kn[l] = v
            wait_ops[o.idx] = wl
            c = dict(kn)
            c[o.lane] = o.val
            clocks[o.idx] = c
        cnt = {}
        for o in self.ops:
            if o.sig:
                cnt[o.lane] = cnt.get(o.lane, 0) + o.inc
                o.val = cnt[o.lane]
            else:
                o.val = None
        for o in self.ops:
            o.waits = [(d.lane, d.val) for d in wait_ops[o.idx]]
        self.n_waits = nw

    def lanes(self):
        ls = []
        seen = set()
        for o in self.ops:
            if o.sig and o.lane not in seen:
                seen.add(o.lane)
                ls.append(o.lane)
        return ls

    def emit(self, sems, block):
        per = {q: [] for q in QUEUES}
        for o in self.ops:
            per[o.eng].append(o)

        def run(e, ops):
            for o in ops:
                for (l, v) in o.waits:
                    e.wait_ge(sems[l], v)
                ins = o.fn(e)
                if o.sig:
                    ins.then_inc(sems[o.lane], o.inc)

        block.tensor(lambda e: run(e, per["pe"]))
        block.scalar(lambda e: run(e, per["act"]))
        block.vector(lambda e: run(e, per["dve"]))
        block.gpsimd(lambda e: run(e, per["pool"]))
        block.sync(lambda e: run(e, per["sp"]))


def build(tile_limit=None, phase_limit=99):
    nc = bass.Bass("TRN2", target_bir_lowering=False)
    es = ExitStack()
    P = Prog()

    def dram_in(name, shape, dt=F32):
        return nc.dram_tensor(name, shape, dt, kind="ExternalInput").ap()

    def dram_out(name, shape, dt=F32):
        return nc.dram_tensor(name, shape, dt, kind="ExternalOutput").ap()

    xT = dram_in("xT", [D, NTOK])
    wsrc = dram_in("wsrc", [NWT, 128, 4096])
    wf_src = dram_in("wf", [128, 128])
    vec_src = dram_in("vecs", [128, 65])
    bf_src = dram_in("b_f", [1, 16])
    st0_src = dram_in("state0", [8, 128, 128])
    ktp_src = dram_in("kT_past", [16, 64, PAST])
    vp_src = dram_in("v_past", [16, 128, 32 * 64])
    lfp_src = dram_in("logf_past", [128, 32 * 16])
    wbf = nc.dram_tensor("wbf", [NWT, 128, 4096], BF16, kind="Internal").ap()
    y_out = dram_out("y", [NTOK, D])
    k_out = dram_out("ko", [NTOK, D])
    v_out = dram_out("vo", [NTOK, D])
    lf_out = dram_out("lfo", [NTOK, 16])
    st_out = dram_out("st", [3, 8, 128, 128])
    out_keys = []
    import os
    DEBUG = int(os.environ.get("KDEBUG", "0"))
    NOINTER = int(os.environ.get("NOINTER", "0"))
    STQ = os.environ.get("STQ", "act")
    CASTENG = os.environ.get("CASTENG", "pool")
    NORMSPLIT = int(os.environ.get("NORMSPLIT", "0"))
    GBANK = int(os.environ.get("GBANK", "1"))
    UBANKS = [int(v) for v in os.environ.get("UBANKS", "3,7").split(",")]
    _zs = [float(v) for v in os.environ.get("ZSCHED", "0,0.6,0,1,0,0.7").split(",")]
    ZSCHED = [(_zs[0], _zs[1]), (_zs[2], _zs[3]), (_zs[4], _zs[5])]
    KTOKENG = os.environ.get("KTOKENG", "dve")
    dbg_t = {}

    def dbg_dump(name, ap_fn, shape, keys, dt=F32):
        if not DEBUG:
            return
        t = nc.dram_tensor("dbg_" + name, shape, dt, kind="ExternalOutput").ap()
        P.dma("sp", ("dbg", name), lambda e: e.dma_start(out=t, in_=ap_fn()), reads=keys, writes=[("out", len(out_keys))])
        out_keys.append(("out", len(out_keys)))

    def sb(name, shape, dt=F32):
        return es.enter_context(nc.sbuf_tensor(name, shape, dt))

    def psum(name, shape, dt=F32):
        return es.enter_context(nc.psum_tensor(name, shape, dt))

    with es:
        x32 = sb("x32", [128, NCH, TP])
        actA = sb("actA", [128, NCH, TP], BF16)
        actB = sb("actB", [128, NCH, TP], BF16)
        wring = sb("wring", [128, NW, 4096], BF16)
        KT = sb("KT", [128, 8, SEQ], BF16)
        Vc = sb("Vc", [128, 16, 16 * 66], BF16)
        hid = sb("hid", [128, 32 * 256], F32)
        S32 = sb("S32", [128, 2, 8, 128])
        Sbf = sb("Sbf", [128, 8, 8, 128], BF16)
        rstd = sb("rstd", [128, TP])
        identb = sb("identb", [128, 128], BF16)
        ident32 = sb("ident32", [128, 128])
        onesb = sb("onesb", [128, 128], BF16)
        ones32 = sb("ones32", [128, 128])
        TU32 = sb("TU32", [128, 128])
        SU32 = sb("SU32", [128, 128])
        maskBD = sb("maskBD", [128, 128])
        negmask = sb("negmask", [128, 128], BF16)
        Sel = sb("Sel", [128, 16, 128], BF16)
        scanmask = sb("scanmask", [128, TP])
        zerosb = sb("zerosb", [128, 512], BF16)
        vec = sb("vec", [128, 65])
        lbt = sb("lbt", [128, 8])
        omlt = sb("omlt", [128, 8])
        nomlt = sb("nomlt", [128, 8])
        lbtmp = sb("lbtmp", [128, 8])
        bfb = sb("bfb", [128, 16])
        wf32 = sb("wf32", [128, 128])
        wfb = sb("wfb", [128, 8, 16], BF16)
        carry = sb("carry", [128, 16])
        cs = sb("cs", [128, 4, 16])
        nGk = sb("nGk", [128, 16, 16])
        lf32 = sb("lf32", [128, 4, 16])
        pl32 = sb("pl32", [128, 4, 16])
        G32 = sb("G32", [128, 4, 16])
        GT = sb("GT", [128, TP], BF16)
        rl = sb("rl", [128, 2, 4])
        PTr = sb("PTr", [128, 3, TP], BF16)

        PB = {i: psum("pb%d" % i, [128, 512]) for i in range(8)}
        PBb = {i: PB[i][:, :].bitcast(BF16) for i in range(8)}
        bank5 = PBb[5]

        def hv(j0, n, dt=BF16):
            a = hid[:, j0 * 256:(j0 + n) * 256]
            if dt == BF16:
                a = a.bitcast(BF16)
            return a

        def hk(j0, n):
            return [("h", j) for j in range(j0, j0 + n)]

        def cst(eng, fn, writes, reads=()):
            P.op(eng, fn, reads=reads, writes=writes)

        cst("pool", lambda e: e.memset(ident32[:], 0.0), ["ident32"])
        cst("pool", lambda e: e.affine_select(out=ident32[:], in_=ident32[:], pattern=[[-1, 128]],
                                              compare_op=ALU.not_equal, fill=1.0, base=0, channel_multiplier=1),
            ["ident32"], ["ident32"])
        cst("pool", lambda e: e.tensor_copy(out=identb[:], in_=ident32[:]), ["identb"], ["ident32"])
        cst("pool", lambda e: e.memset(ones32[:], 1.0), ["ones32"])
        cst("pool", lambda e: e.memset(onesb[:], 1.0), ["onesb"])
        cst("pool", lambda e: e.memset(zerosb[:], 0.0), ["zerosb"])
        cst("pool", lambda e: e.affine_select(out=TU32[:], in_=ones32[:], pattern=[[1, 128]],
                                              compare_op=ALU.is_ge, fill=0.0, base=0, channel_multiplier=-1),
            ["TU32"], ["ones32"])
        cst("pool", lambda e: e.affine_select(out=SU32[:], in_=ones32[:], pattern=[[-1, 128]],
                                              compare_op=ALU.is_gt, fill=0.0, base=0, channel_multiplier=1),
            ["SU32"], ["ones32"])
        cst("pool", lambda e: e.tensor_copy(out=maskBD[:], in_=TU32[:]), ["maskBD"], ["TU32"])
        cst("pool", lambda e: e.memset(maskBD[0:64, 64:128], 0.0), ["maskBD"], ["maskBD"])
        cst("pool", lambda e: e.tensor_scalar(out=negmask[:], in0=SU32[:], scalar1=NEG, scalar2=None, op0=ALU.mult),
            ["negmask"], ["SU32"])
        cst("pool", lambda e: e.memset(Sel[:], 0.0), ["Sel"])
        cst("pool", lambda e: e.memset(GT[:], 0.0), ["GT"])
        cst("pool", lambda e: e.affine_select(out=Sel[0:16], in_=Sel[0:16], pattern=[[-1, 16], [0, 128]],
                                              compare_op=ALU.not_equal, fill=1.0, base=0, channel_multiplier=1),
            ["Sel"], ["Sel"])
        cst("pool", lambda e: e.memset(scanmask[:], 1.0), ["scanmask"])
        cst("pool", lambda e: e.memset(scanmask[:].rearrange("p (c t) -> p c t", t=64)[:, :, 0:1], 0.0),
            ["scanmask"], ["scanmask"])
        cst("pool", lambda e: e.memset(Vc[:].rearrange("p k (h d) -> p (k h) d", d=66)[:, :, 64:65], 1.0),
            [("V", kb) for kb in range(16)])
        P.dma("sp", "vec", lambda e: e.dma_start(out=vec[:], in_=vec_src), writes=["vec"])
        P.dma("sp", "bfb", lambda e: e.dma_start(out=bfb[:], in_=bf_src.partition_broadcast(128)), writes=["bfb"])
        P.dma("sp", "wf32", lambda e: e.dma_start(out=wf32[:], in_=wf_src), writes=["wf32"])
        cst("dve", lambda e: e.tensor_copy(out=wfb[:].rearrange("p a b -> p (a b)"), in_=wf32[:]), ["wfb"], ["wf32"])
        cst("dve", lambda e: e.tensor_tensor(out=lbtmp[:], in0=vec[:, 48:56], in1=vec[:, 56:64], op=ALU.subtract),
            ["lbtmp"], ["vec"])
        cst("act", lambda e: e.activation(out=lbt[:], in_=lbtmp[:], func=AF.Sigmoid), ["lbt"], ["lbtmp"])
        cst("dve", lambda e: e.tensor_scalar(out=omlt[:], in0=lbt[:], scalar1=-1.0, scalar2=1.0, op0=ALU.mult, op1=ALU.add),
            ["omlt"], ["lbt"])
        cst("dve", lambda e: e.tensor_scalar(out=nomlt[:], in0=lbt[:], scalar1=1.0, scalar2=-1.0, op0=ALU.mult, op1=ALU.add),
            ["nomlt"], ["lbt"])

        epsb = sb("epsb", [128, 1])
        cst("pool", lambda e: e.memset(epsb[:], EPS), ["epsb"])

        for j in range(NWT):
            P.dma("pool", ("wc", j),
                  lambda e, j=j: e.dma_start(out=wbf[j].rearrange("p (a b) -> p a b", b=2048),
                                             in_=wsrc[j].rearrange("p (a b) -> p a b", b=2048)),
                  writes=[("wbf", j)])

        wstate = {"n": 0}

        def wload(j):
            s = wstate["n"] % NW
            wstate["n"] += 1
            P.dma("sp", ("w", s), lambda e: e.dma_start(out=wring[:, s, :], in_=wbf[j]),
                  reads=[("wbf", j)], writes=[("w", s)])
            return wring[:, s, :], ("w", s)

        def bk(b):
            if b == 7:
                return [("pb7", i) for i in range(4)]
            if b == 5:
                return [("pb5", 0), ("pb5", 1)]
            return [("pb", b)]

        def mm_group(bank_keys, out_ap, items):
            n = len(items)
            for i, (l, r, rk) in enumerate(items):
                P.op("pe", lambda e, l=l, r=r, i=i: e.matmul(out_ap, lhsT=l, rhs=r, start=(i == 0), stop=(i == n - 1)),
                     reads=rk, writes=list(bank_keys))

        def norm(T, gcol, dst, dkey, bank):
            for c in range(NCH):
                if NORMSPLIT and c % 2 == 1:
                    P.op("dve", lambda e, c=c: e.tensor_tensor(out=dst[:, c, 0:T], in0=x32[:, c, 0:T], in1=x32[:, c, 0:T], op=ALU.mult),
                         reads=[("x", c)], writes=[(dkey, c)])
                else:
                    P.op("act", lambda e, c=c: e.activation(out=dst[:, c, 0:T], in_=x32[:, c, 0:T], func=AF.Square),
                         reads=[("x", c)], writes=[(dkey, c)])
            bkk = bk(bank)
            mm_group(bkk, PB[bank][:, 0:T], [(onesb[:], dst[:, c, 0:T], ["onesb", (dkey, c)]) for c in range(NCH)])
            P.op("act", lambda e: e.activation(out=rstd[:, 0:T], in_=PB[bank][:, 0:T], func=AF.Ln, bias=epsb[:], scale=1.0 / D),
                 reads=bkk + ["epsb"], writes=["rstd"])
            P.op("act", lambda e: e.activation(out=rstd[:, 0:T], in_=rstd[:, 0:T], func=AF.Exp, scale=-0.5),
                 reads=["rstd"], writes=["rstd"])
            for c in range(NCH):
                P.op("dve", lambda e, c=c: e.scalar_tensor_tensor(out=dst[:, c, 0:T], in0=x32[:, c, 0:T],
                                                                 scalar=vec[:, gcol + c:gcol + c + 1], in1=rstd[:, 0:T],
                                                                 op0=ALU.mult, op1=ALU.mult),
                     reads=[("x", c), "vec", "rstd"], writes=[(dkey, c)])


        def proj_residual(T, wtiles, src, skey, nkc, bank_rot):
            cw = 4096 // nkc
            npc = cw // 128
            c = 0
            for j in wtiles:
                wt, wk = wload(j)
                wv = wt.rearrange("p (k n) -> p k n", n=cw)
                for q in range(npc):
                    b = bank_rot[c % len(bank_rot)]
                    bkk = bk(b)
                    mm_group(bkk, PB[b][:, 0:T],
                             [(wv[:, kc, q * 128:(q + 1) * 128], src(kc)[:, 0:T], [wk, skey(kc)]) for kc in range(nkc)])
                    P.op("dve", lambda e, c=c, b=b: e.tensor_tensor(out=x32[:, c, 0:T], in0=x32[:, c, 0:T], in1=PB[b][:, 0:T], op=ALU.add),
                         reads=[("x", c)] + bkk, writes=[("x", c)])
                    c += 1

        def mlp(T, l, src, skey):
            up0 = 10 if l == 0 else 36
            dn0 = up0 + 8
            hview = hv(0, 32).rearrange("p (c t) -> p c t", t=512)
            n = 0
            for j in range(8):
                wt, wk = wload(up0 + j)
                wv = wt.rearrange("p (k n) -> p k n", n=512)
                for q in range(4):
                    b = n % 4
                    mm_group(bk(b), PB[b][:, 0:T],
                             [(wv[:, kc, q * 128:(q + 1) * 128], src[:, kc, 0:T], [wk, (skey, kc)]) for kc in range(NCH)])
                    P.op("act", lambda e, n=n, b=b: e.activation(out=hview[:, n, 0:T], in_=PB[b][:, 0:T], func=AF.Relu),
                         reads=bk(b), writes=[("h", n)])
                    P.op("pool", lambda e, n=n: e.tensor_tensor(out=hview[:, n, 0:T], in0=hview[:, n, 0:T], in1=hview[:, n, 0:T], op=ALU.mult),
                         reads=[("h", n)], writes=[("h", n)])
                    n += 1
            proj_residual(T, list(range(dn0, dn0 + 8)), lambda kc: hview[:, kc, :], lambda kc: ("h", kc), 32, [4, 6, 7])

        S_par = {}

        def capture(fn):
            saved = P.ops
            P.ops = []
            try:
                fn()
                out = P.ops
            finally:
                P.ops = saved
            return out

        def zipper(LA, LB):
            out = []
            ia = ib = 0
            na, nb = len(LA), len(LB)
            while ia < na or ib < nb:
                if ib >= nb or (ia < na and ia * nb <= ib * na):
                    out.append(LA[ia]); ia += 1
                else:
                    out.append(LB[ib]); ib += 1
            return out

        def zipperN(lists, sched=None):
            idxs = [i for i, L in enumerate(lists) if L]
            if sched is None:
                sched = [(0.0, 1.0)] * len(lists)
            items = []
            for i in idxs:
                L = lists[i]
                off, sc = sched[i]
                for k, o in enumerate(L):
                    items.append((off + sc * k / len(L), i, k, o))
            items.sort(key=lambda t: (t[0], t[1], t[2]))
            return [t[3] for t in items]

        def sigmoid_chain(dst, src_psum, src_keys, dst_keys, part=None):
            if part in (None, 0):
                P.op("act", lambda e: e.activation(out=dst, in_=src_psum, func=AF.Exp, scale=-1.0), reads=src_keys, writes=dst_keys)
            if part in (None, 1):
                P.op("act", lambda e: e.activation(out=dst, in_=dst, func=AF.Ln, bias=1.0, scale=1.0), reads=dst_keys, writes=dst_keys)
                P.op("act", lambda e: e.activation(out=dst, in_=dst, func=AF.Exp, scale=-1.0), reads=dst_keys, writes=dst_keys)

        def run_hgrn(T, seq_first_tile, sample):
            nblk = max(1, T // 128)
            tb = min(128, T)
            nchk = T // 64
            fbuf = hv(13, 2, F32); kk = hv(15, 1); bc = hv(16, 2, F32); kinv = hv(18, 1); kend = hv(19, 1); scT = hv(20, 1)
            rs2 = hv(29, 2, F32); osq = hv(31, 1); onb = osq
            K = {"f": hk(13, 2), "kk": hk(15, 1), "bc": hk(16, 2), "kinv": hk(18, 1), "kend": hk(19, 1), "scT": hk(20, 1),
                 "rs2": hk(29, 2), "osq": hk(31, 1)}

            def b3(h):
                r = h % 3
                return dict(sq=hv(r, 1), Ksq=hk(r, 1), sg=hv(7 + 2 * r, 1), Ksg=hk(7 + 2 * r, 1),
                            vtok=hv(8 + 2 * r, 1), Kvtok=hk(8 + 2 * r, 1))

            def b2(h):
                p = h % 2
                return dict(sz=hv(3 + 2 * p, 2, F32), Ksz=hk(3 + 2 * p, 2),
                            qdec=hv(21 + 4 * p, 1), Kqdec=hk(21 + 4 * p, 1), ktok=hv(22 + 4 * p, 1), Kktok=hk(22 + 4 * p, 1),
                            eb=hv(23 + 4 * p, 2, F32), Keb=hk(23 + 4 * p, 2))

            P.tag = "hgrn.norm"
            norm(T, 0, actA, "A", 0)

            def stageA1(h):
                X3, X2 = b3(h), b2(h)
                sq, sg, vtok, sz = X3["sq"], X3["sg"], X3["vtok"], X2["sz"]
                P.tag = "hgrn.A.proj"
                wt, wk = wload(h)
                wv = wt.rearrange("p (k n) -> p k n", n=512)
                xs = lambda kc: actA[:, kc, 0:T]
                xk = lambda kc: [wk, ("A", kc)]
                mm_group(bk(0), PB[0][:, 0:T], [(wv[:, kc, 0:128], xs(kc), xk(kc)) for kc in range(NCH)])
                mm_group(bk(1), PB[1][:, 0:T], [(wv[:, kc, 128:256], xs(kc), xk(kc)) for kc in range(NCH)])
                sigmoid_chain(sq[:, 0:T], PB[0][:, 0:T], bk(0), X3["Ksq"], part=0)
                sigmoid_chain(sz[:, 0:T], PB[1][:, 0:T], bk(1), X2["Ksz"], part=0)
                sigmoid_chain(sq[:, 0:T], PB[0][:, 0:T], bk(0), X3["Ksq"], part=1)
                P.op("dve", lambda e: e.tensor_tensor(out=sq[:, 0:T], in0=PB[0][:, 0:T], in1=sq[:, 0:T], op=ALU.mult), reads=bk(0) + X3["Ksq"], writes=X3["Ksq"])
                sigmoid_chain(sz[:, 0:T], PB[1][:, 0:T], bk(1), X2["Ksz"], part=1)
                mm_group(bk(GBANK), PB[GBANK][:, 0:T], [(wv[:, kc, 384:512], xs(kc), xk(kc)) for kc in range(NCH)])
                sigmoid_chain(sg[:, 0:T], PB[GBANK][:, 0:T], bk(GBANK), X3["Ksg"])
                for bl in range(nblk):
                    mm_group(bk(2), PB[2][0:tb, bl * 128:(bl + 1) * 128],
                             [(actA[:, kc, bl * tb:(bl + 1) * tb], wv[:, kc, 256:384], xk(kc)) for kc in range(NCH)])
                P.op("dve", lambda e: e.tensor_copy(out=vtok[0:tb, 0:nblk * 128], in_=PB[2][0:tb, 0:nblk * 128]),
                     reads=bk(2), writes=X3["Kvtok"])

            def stageA2(h):
                X3, X2 = b3(h), b2(h)
                sq, vtok, sz = X3["sq"], X3["vtok"], X2["sz"]
                qdec, ktok, ebuf = X2["qdec"], X2["ktok"], X2["eb"]
                einv = fbuf
                ob = 5 + (h % 2)
                P.tag = "hgrn.A.sc"
                P.op("dve", lambda e: e.tensor_scalar(out=fbuf[:, 0:T], in0=sz[:, 0:T], scalar1=omlt[:, h:h + 1], scalar2=lbt[:, h:h + 1],
                                                     op0=ALU.mult, op1=ALU.add), reads=X2["Ksz"] + ["omlt", "lbt"], writes=K["f"])
                P.op("dve", lambda e: e.tensor_scalar(out=kk[:, 0:T], in0=sz[:, 0:T], scalar1=nomlt[:, h:h + 1], scalar2=omlt[:, h:h + 1],
                                                     op0=ALU.mult, op1=ALU.add), reads=X2["Ksz"] + ["omlt", "nomlt"], writes=K["kk"])
                P.op("act", lambda e: e.activation(out=fbuf[:, 0:T], in_=fbuf[:, 0:T], func=AF.Ln), reads=K["f"], writes=K["f"])
                P.op("dve", lambda e: e.tensor_tensor_scan(out=bc[:, 0:T], data0=scanmask[:, 0:T], data1=fbuf[:, 0:T], initial=0.0,
                                                           op0=ALU.mult, op1=ALU.add), reads=K["f"] + ["scanmask"], writes=K["bc"])
                P.op("act", lambda e: e.activation(out=ebuf[:, 0:T], in_=bc[:, 0:T], func=AF.Exp), reads=K["bc"], writes=X2["Keb"])
                P.op("act", lambda e: e.activation(out=einv[:, 0:T], in_=bc[:, 0:T], func=AF.Exp, scale=-1.0), reads=K["bc"], writes=K["f"])
                P.op("dve", lambda e: e.tensor_tensor(out=qdec[:, 0:T], in0=sq[:, 0:T], in1=ebuf[:, 0:T], op=ALU.mult),
                     reads=X3["Ksq"] + X2["Keb"], writes=X2["Kqdec"])
                P.op("dve", lambda e: e.tensor_tensor(out=kinv[:, 0:T], in0=kk[:, 0:T], in1=einv[:, 0:T], op=ALU.mult),
                     reads=K["kk"] + K["f"], writes=K["kinv"])
                P.op("pool", lambda e: e.tensor_tensor(
                    out=kend[:, 0:T].rearrange("p (c t) -> p c t", t=64),
                    in0=kinv[:, 0:T].rearrange("p (c t) -> p c t", t=64),
                    in1=ebuf[:, 0:T].rearrange("p (c t) -> p c t", t=64)[:, :, 63:64].to_broadcast([128, nchk, 64]), op=ALU.mult),
                    reads=K["kinv"] + X2["Keb"], writes=K["kend"])
                for bl in range(nblk):
                    P.op("pe", lambda e, bl=bl: e.matmul(PB[4][0:tb, bl * 128:bl * 128 + tb], lhsT=kinv[:, bl * tb:(bl + 1) * tb],
                                                         rhs=qdec[:, bl * tb:(bl + 1) * tb], start=True, stop=True),
                         reads=K["kinv"] + X2["Kqdec"], writes=bk(4))
                if T >= 128:
                    P.op("dve", lambda e: e.tensor_tensor(out=scT[:, 0:T].rearrange("p (b t) -> p b t", t=128),
                                                          in0=PB[4][:, 0:T].rearrange("p (b t) -> p b t", t=128),
                                                          in1=maskBD[:].unsqueeze(1).to_broadcast([128, nblk, 128]), op=ALU.mult),
                         reads=bk(4) + ["maskBD"], writes=K["scT"])
                else:
                    P.op("dve", lambda e: e.tensor_tensor(out=scT[0:tb, 0:tb], in0=PB[4][0:tb, 0:tb], in1=maskBD[0:tb, 0:tb], op=ALU.mult),
                         reads=bk(4) + ["maskBD"], writes=K["scT"])
                for bl in range(nblk):
                    P.op("pe", lambda e, bl=bl: e.transpose(out=PBb[4][0:tb, bl * 128:(bl + 1) * 128], in_=kend[:, bl * tb:(bl + 1) * tb],
                                                            identity=identb[:]),
                         reads=K["kend"] + ["identb"], writes=bk(4))
                if KTOKENG == "act":
                    P.op("act", lambda e: e.copy(out=ktok[0:tb, 0:nblk * 128], in_=PBb[4][0:tb, 0:nblk * 128]), reads=bk(4), writes=X2["Kktok"])
                else:
                    P.op("dve", lambda e: e.tensor_copy(out=ktok[0:tb, 0:nblk * 128], in_=PBb[4][0:tb, 0:nblk * 128]), reads=bk(4), writes=X2["Kktok"])
                P.tag = "hgrn.A.intra"
                for bl in range(nblk):
                    P.op("pe", lambda e, bl=bl: e.matmul(PB[ob][:, bl * tb:(bl + 1) * tb], lhsT=vtok[0:tb, bl * 128:(bl + 1) * 128],
                                                         rhs=scT[0:tb, bl * 128:bl * 128 + tb], start=(bl == 0), stop=False),
                         reads=X3["Kvtok"] + K["scT"], writes=bk(ob))

            def stageB(h):
                X3, X2 = b3(h), b2(h)
                vtok = X3["vtok"]
                qdec, ktok, ebuf = X2["qdec"], X2["ktok"], X2["eb"]
                ob = 5 + (h % 2)
                P.tag = "hgrn.B.chain"
                for c in range(nchk):
                    bl, half = divmod(c, 2)
                    p0 = half * 64
                    ub = UBANKS[c % len(UBANKS)]
                    P.op("pe", lambda e, bl=bl, p0=p0, ub=ub: e.matmul(PB[ub][:, 0:128],
                                                                      lhsT=ktok[p0:p0 + 64, bl * 128:(bl + 1) * 128],
                                                                      rhs=vtok[p0:p0 + 64, bl * 128:(bl + 1) * 128], start=True, stop=True),
                         reads=X2["Kktok"] + X3["Kvtok"], writes=bk(ub))
                    pi = S_par.get(h, 0)
                    po = 1 - pi
                    S_par[h] = po
                    ecol = c * 64 + 63
                    P.op("dve", lambda e, pi=pi, po=po, ub=ub, ecol=ecol: e.scalar_tensor_tensor(
                        out=S32[:, po, h, :], in0=S32[:, pi, h, :], scalar=ebuf[:, ecol:ecol + 1],
                        in1=PB[ub][:, 0:128], op0=ALU.mult, op1=ALU.add),
                        reads=[("S32", pi, h)] + bk(ub) + X2["Keb"], writes=[("S32", po, h)])
                    if c + 1 < nchk:
                        P.op(CASTENG, (lambda e, po=po, c=c: e.tensor_copy(out=Sbf[:, h, c + 1, :], in_=S32[:, po, h, :])) if CASTENG != "act" else
                             (lambda e, po=po, c=c: e.copy(out=Sbf[:, h, c + 1, :], in_=S32[:, po, h, :])),
                             reads=[("S32", po, h)], writes=[("Sbf", h, c + 1)])
                for c in range(nchk):
                    last = (c == nchk - 1)
                    P.op("pe", lambda e, c=c, last=last: e.matmul(PB[ob][:, c * 64:(c + 1) * 64], lhsT=Sbf[:, h, c, :],
                                                                 rhs=qdec[:, c * 64:(c + 1) * 64], start=False, stop=last),
                         reads=[("Sbf", h, c)] + X2["Kqdec"], writes=bk(ob))
                pf = S_par[h]
                P.op(CASTENG, (lambda e, pf=pf: e.tensor_copy(out=Sbf[:, h, 0, :], in_=S32[:, pf, h, :])) if CASTENG != "act" else
                     (lambda e, pf=pf: e.copy(out=Sbf[:, h, 0, :], in_=S32[:, pf, h, :])),
                     reads=[("S32", pf, h)], writes=[("Sbf", h, 0)])

            def stageBn(h):
                X3 = b3(h)
                sg = X3["sg"]
                ob = 5 + (h % 2)
                P.tag = "hgrn.B.norm"
                P.op("act", lambda e: e.activation(out=osq[:, 0:T], in_=PB[ob][:, 0:T], func=AF.Square), reads=bk(ob), writes=K["osq"])
                P.op("pe", lambda e: e.matmul(PB[7][:, 0:T], lhsT=onesb[:], rhs=osq[:, 0:T], start=True, stop=True),
                     reads=K["osq"] + ["onesb"], writes=bk(7))
                P.op("act", lambda e: e.activation(out=rs2[:, 0:T], in_=PB[7][:, 0:T], func=AF.Ln, bias=epsb[:], scale=1.0 / 128),
                     reads=bk(7) + ["epsb"], writes=K["rs2"])
                P.op("act", lambda e: e.activation(out=rs2[:, 0:T], in_=rs2[:, 0:T], func=AF.Exp, scale=-0.5), reads=K["rs2"], writes=K["rs2"])
                P.op("dve", lambda e: e.scalar_tensor_tensor(out=onb[:, 0:T], in0=PB[ob][:, 0:T], scalar=vec[:, 64:65], in1=rs2[:, 0:T],
                                                             op0=ALU.mult, op1=ALU.mult), reads=bk(ob) + ["vec"] + K["rs2"], writes=K["osq"])
                P.op("pool", lambda e: e.tensor_tensor(out=actB[:, h, 0:T], in0=onb[:, 0:T], in1=sg[:, 0:T], op=ALU.mult),
                     reads=K["osq"] + X3["Ksg"], writes=[("B", h)])

            P.ops.extend(capture(lambda: stageA1(0)))
            P.ops.extend(capture(lambda: stageA1(1)))
            P.ops.extend(capture(lambda: stageA2(0)))
            for i in range(8):
                Y = capture(lambda: stageA1(i + 2)) if i + 2 < 8 else []
                X = capture(lambda: stageA2(i + 1)) if i + 1 < 8 else []
                Z = capture(lambda: stageB(i))
                N = capture(lambda: stageBn(i))
                P.ops.extend(zipperN([Y, X, Z], ZSCHED))
                P.ops.extend(N)
            P.tag = "hgrn.wo"
            proj_residual(T, [8, 9], lambda kc: actB[:, kc, :], lambda kc: ("B", kc), 8, [1, 2])

        def run_kv(T, tok0, kt_dst, kt_keys, v_dst, v_keys, nG_dst, nG_key, first):
            nblk = max(1, T // 128)
            tb = min(128, T)
            norm(T, 16, actA, "A", 0)
            stage = [hv(2 * i, 2, F32) for i in range(4)]
            skeys = [hk(2 * i, 2) for i in range(4)]
            si = {"n": 0}
            import os
            kvs = int(os.environ.get("KVSTOP", "9"))
            if kvs < -2:
                return
            wts = [wload(26), wload(27)]
            for pr in range(8):
                wt, wk = wts[pr // 4]
                wv = wt.rearrange("p (k n) -> p k n", n=512)
                b = pr % 2
                mm_group(bk(b), PB[b][:, 0:T],
                         [(wv[:, kc, (pr % 4) * 128:(pr % 4 + 1) * 128], actA[:, kc, 0:T], [wk, ("A", kc)]) for kc in range(NCH)])
                eng = "act" if pr % 2 == 0 else "dve"
                if eng == "act":
                    P.op("act", lambda e, pr=pr, b=b: e.copy(out=kt_dst(pr), in_=PB[b][:, 0:T]), reads=[("pb", b)], writes=kt_keys(pr))
                else:
                    P.op("dve", lambda e, pr=pr, b=b: e.tensor_copy(out=kt_dst(pr), in_=PB[b][:, 0:T]), reads=[("pb", b)], writes=kt_keys(pr))

            if kvs < -1:
                return

            def tokmajor(wtile, dst_dram, extra):
                wt, wk = wtile
                wv = wt.rearrange("p (k n) -> p k n", n=512)
                for bl in range(nblk):
                    b = 2 + (si["n"] % 2)
                    s = si["n"] % 4
                    si["n"] += 1
                    mm_group(bk(b), PB[b][0:tb, :],
                             [(actA[:, kc, bl * tb:(bl + 1) * tb], wv[:, kc, :], [wk, ("A", kc)]) for kc in range(NCH)])
                    P.op("act", lambda e, b=b, s=s: e.copy(out=stage[s][0:tb, :], in_=PB[b][0:tb, :]), reads=[("pb", b)], writes=skeys[s])
                    extra(bl, b)
                    P.dma(STQ, ("st", s), lambda e, s=s, bl=bl: e.dma_start(out=dst_dram(bl), in_=stage[s][0:tb, :]),
                          reads=skeys[s], writes=[("out", len(out_keys))])
                    out_keys.append(("out", len(out_keys)))

            for half in range(2):
                tokmajor(wts[half], lambda bl, half=half: k_out[tok0 + bl * tb: tok0 + (bl + 1) * tb, half * 512:(half + 1) * 512],
                         lambda bl, b: None)
            if kvs < 0:
                return
            vts = [wload(28), wload(29)]
            for half in range(2):
                def vextra(bl, b, half=half):
                    s_ = (si["n"] - 1) % 4
                    P.op("dve", lambda e: e.tensor_copy(out=v_dst(bl)[:, half * 8:(half + 1) * 8, 0:64],
                                                        in_=stage[s_][0:tb, :].rearrange("p (h d) -> p h d", d=64)),
                         reads=skeys[s_], writes=v_keys(bl))
                tokmajor(vts[half], lambda bl, half=half: v_out[tok0 + bl * tb: tok0 + (bl + 1) * tb, half * 512:(half + 1) * 512], vextra)
            if kvs < 1:
                return
            for bl in range(nblk):
                mm_group(bk(4), PB[4][0:tb, bl * 16:(bl + 1) * 16],
                         [(actA[:, kc, bl * tb:(bl + 1) * tb], wfb[:, kc, :], ["wfb", ("A", kc)]) for kc in range(NCH)])
            nb16 = nblk * 16
            plv = pl32[:].rearrange("p a b -> p (a b)")
            lfv = lf32[:].rearrange("p a b -> p (a b)")
            P.op("dve", lambda e: e.tensor_tensor(out=pl32[0:tb, 0:nblk, :], in0=PB[4][0:tb, 0:nb16].rearrange("p (a b) -> p a b", b=16),
                                                  in1=bfb[0:tb, :].unsqueeze(1).to_broadcast([tb, nblk, 16]), op=ALU.add),
                 reads=[("pb", 4), "bfb"], writes=["pl32"])
            P.op("act", lambda e: e.activation(out=plv[0:tb, 0:nb16], in_=plv[0:tb, 0:nb16], func=AF.Exp, scale=-1.0), reads=["pl32"], writes=["pl32"])
            P.op("act", lambda e: e.activation(out=plv[0:tb, 0:nb16], in_=plv[0:tb, 0:nb16], func=AF.Ln, bias=1.0, scale=1.0), reads=["pl32"], writes=["pl32"])
            P.op("dve", lambda e: e.tensor_scalar(out=lfv[0:tb, 0:nb16], in0=plv[0:tb, 0:nb16], scalar1=-1.0, scalar2=None, op0=ALU.mult),
                 reads=["pl32"], writes=["lf32"])
            P.dma(STQ, "lfo", lambda e: e.dma_start(out=lf_out[tok0:tok0 + T, :].rearrange("(b p) h -> p b h", p=tb), in_=lf32[0:tb, 0:nblk, :]),
                  reads=["lf32"], writes=[("out", len(out_keys))])
            out_keys.append(("out", len(out_keys)))
            if kvs < 2:
                return
            if first:
                P.op("dve", lambda e: e.memset(carry[:], 0.0), writes=["carry"])
            P.op("pe", lambda e: e.matmul(PB[6][0:tb, 0:nb16], lhsT=TU32[0:tb, 0:tb], rhs=lfv[0:tb, 0:nb16], start=True, stop=True),
                 reads=["TU32", "lf32"], writes=[("pb", 6)])
            P.op("pe", lambda e: e.matmul(PB[7][0:tb, 0:nb16], lhsT=ones32[0:tb, 0:tb], rhs=lfv[0:tb, 0:nb16], start=True, stop=True),
                 reads=["ones32", "lf32"], writes=[("pb7", 0), ("pb7", 1), ("pb7", 2), ("pb7", 3)])
            P.op("dve", lambda e: e.tensor_copy(out=cs[0:tb, 0, :], in_=carry[0:tb, :]), reads=["carry"], writes=["cs"])
            for bl in range(nblk):
                dst = cs[0:tb, bl + 1, :] if bl + 1 < nblk else carry[0:tb, :]
                P.op("dve", lambda e, bl=bl, dst=dst: e.tensor_tensor(out=dst, in0=cs[0:tb, bl, :], in1=PB[7][0:tb, bl * 16:(bl + 1) * 16], op=ALU.add),
                     reads=["cs", ("pb7", 0)], writes=(["cs"] if bl + 1 < nblk else ["carry"]))
            P.op("dve", lambda e: e.tensor_tensor(out=G32[0:tb, 0:nblk, :], in0=PB[6][0:tb, 0:nb16].rearrange("p (a b) -> p a b", b=16),
                                                  in1=cs[0:tb, 0:nblk, :], op=ALU.add), reads=[("pb", 6), "cs"], writes=["G32"])
            for bl in range(nblk):
                P.op("dve", lambda e, bl=bl: e.tensor_scalar(out=nG_dst(bl), in0=G32[0:tb, bl, :], scalar1=-1.0, scalar2=None, op0=ALU.mult),
                     reads=["G32"], writes=[nG_key])
            if kvs < 3:
                return
            for bl in range(nblk):
                P.op("pe", lambda e, bl=bl: e.transpose(out=PB[4][0:16, bl * tb:(bl + 1) * tb], in_=G32[0:tb, bl, :], identity=ident32[0:tb, 0:tb]),
                     reads=["G32", "ident32"], writes=[("pb", 4)])
            P.op("dve", lambda e: e.tensor_copy(out=GT[0:16, 0:T], in_=PB[4][0:16, 0:T]), reads=[("pb", 4), "GT"], writes=["GT"])

        def run_attn(T, keyblocks_for_head, qtile_i):
            nblk = max(1, T // 128)
            tb = min(128, T)
            norm(T, 24, actB, "B", 0)
            QTz = hv(0, 16).rearrange("p (h t) -> p h t", t=512)
            qk = lambda h: [("h", h)]
            sgt = hv(16, 8).rearrange("p (b f) -> p b f", f=1024)
            sgk = lambda bl: hk(16 + 2 * bl, 2)
            PT = [PTr[:, i, :] for i in range(3)]
            PTk = [[("PT", i)] for i in range(3)]
            QTp = hv(0, 16).rearrange("p (c two t) -> p c two t", two=2, t=512)
            P.op("pool", lambda e: e.memset(QTp[64:128, :, 0, :], 0.0), writes=[("h", 2 * c) for c in range(8)])
            P.op("pool", lambda e: e.memset(QTp[0:64, :, 1, :], 0.0), writes=[("h", 2 * c + 1) for c in range(8)])
            ats = int(os.environ.get("ATTSTOP", "99"))
            if ats < 1:
                return
            wq = [wload(30), wload(31)]
            for pr in range(8):
                wt, wk = wq[pr // 4]
                wv = wt.rearrange("p (k n) -> p k n", n=512)
                b = pr % 2
                mm_group(bk(b), PB[b][:, 0:T],
                         [(wv[:, kc, (pr % 4) * 128:(pr % 4 + 1) * 128], actB[:, kc, 0:T], [wk, ("B", kc)]) for kc in range(NCH)])
                P.op("dve", lambda e, pr=pr, b=b: e.tensor_scalar(out=QTz[0:64, 2 * pr, 0:T], in0=PB[b][0:64, 0:T], scalar1=0.125, scalar2=None, op0=ALU.mult),
                     reads=bk(b), writes=qk(2 * pr))
                P.op("dve", lambda e, pr=pr, b=b: e.tensor_scalar(out=QTz[64:128, 2 * pr + 1, 0:T], in0=PB[b][64:128, 0:T], scalar1=0.125, scalar2=None, op0=ALU.mult),
                     reads=bk(b), writes=qk(2 * pr + 1))
            if ats < 2:
                return
            wg = [wload(32), wload(33)]
            n = 0
            for half in range(2):
                wt, wk = wg[half]
                wv = wt.rearrange("p (k n) -> p k n", n=512)
                for bl in range(nblk):
                    b = 2 + n % 2
                    n += 1
                    mm_group(bk(b), PB[b][0:tb, :],
                             [(actB[:, kc, bl * tb:(bl + 1) * tb], wv[:, kc, :], [wk, ("B", kc)]) for kc in range(NCH)])
                    sigmoid_chain(sgt[0:tb, bl, half * 512:(half + 1) * 512], PB[b][0:tb, :], bk(b), sgk(bl))
            if ats < 3:
                return
            nheads = int(os.environ.get("ATTHEADS", "16"))
            attpv = int(os.environ.get("ATTPV", "1"))
            attsel = int(os.environ.get("ATTSEL", "1"))
            SKEW = 2
            items = []
            nxt_head = {"h": 0}

            def expand():
                h = nxt_head["h"]
                nxt_head["h"] += 1
                kbs = keyblocks_for_head(h)
                for j, kb in enumerate(kbs):
                    items.append((h, kb, j == 0, j == len(kbs) - 1))

            def emit_S(i):
                h, kb, first, last = items[i]
                nk = kb["nk"]
                dj = kb["diag"]
                c0 = 0 if dj is None else dj * tb
                N = T - c0
                sbank = i % 5
                pti = i % 3
                sk = bk(sbank)
                sc = PB[sbank]
                P.op("pe", lambda e: e.matmul(sc[0:nk, 0:N], lhsT=kb["kt"], rhs=QTz[:, h, c0:T], start=True, stop=False),
                     reads=kb["kt_keys"] + qk(h), writes=sk)
                P.op("pe", lambda e: e.matmul(sc[0:nk, 0:N], lhsT=Sel[:, h, 0:nk], rhs=GT[:, c0:T], start=False, stop=(dj is None)),
                     reads=["Sel", "GT"], writes=sk)
                if dj is not None:
                    P.op("pe", lambda e: e.matmul(sc[0:nk, 0:nk], lhsT=identb[:, 0:nk], rhs=negmask[:, 0:nk], start=False, stop=True),
                         reads=["identb", "negmask"], writes=sk)
                P.op("act", lambda e: e.activation(out=PT[pti][0:nk, 0:N], in_=sc[0:nk, 0:N], func=AF.Exp, bias=kb["bias"], scale=1.0),
                     reads=sk + kb["bias_keys"], writes=PTk[pti])

            def emit_PV(i):
                h, kb, first, last = items[i]
                nk = kb["nk"]
                dj = kb["diag"]
                c0 = 0 if dj is None else dj * tb
                pti = i % 3
                ab = 6 + (h % 2)
                abk = bk(ab)
                acc = PB[ab][:, 0:nblk * 65].rearrange("p (b d) -> p b d", d=65)
                if first:
                    P.op("pe", lambda e: e.matmul(PB[ab][0:tb, 0:nblk * 65], lhsT=zerosb[:, 0:tb], rhs=zerosb[:, 0:nblk * 65], start=True, stop=False),
                         reads=["zerosb"], writes=abk)
                for bl in range(nblk):
                    if bl * tb < c0:
                        continue
                    lastmm = last and bl == nblk - 1
                    P.op("pe", lambda e, bl=bl, lastmm=lastmm: e.matmul(
                        acc[0:tb, bl, :], lhsT=PT[pti][0:nk, bl * tb - c0:bl * tb - c0 + tb], rhs=kb["v"], start=False, stop=lastmm),
                        reads=PTk[pti] + kb["v_keys"], writes=abk)
                if last:
                    ri = h % 2
                    P.op("dve", lambda e: e.reciprocal(out=rl[0:tb, ri, 0:nblk], in_=acc[0:tb, :, 64]), reads=abk, writes=[("rl", ri)])
                    for bl in range(nblk):
                        P.op("dve", lambda e, bl=bl: e.scalar_tensor_tensor(
                            out=sgt[0:tb, bl, h * 64:(h + 1) * 64], in0=acc[0:tb, bl, 0:64], scalar=rl[0:tb, ri, bl:bl + 1],
                            in1=sgt[0:tb, bl, h * 64:(h + 1) * 64], op0=ALU.mult, op1=ALU.mult),
                            reads=abk + [("rl", ri)] + sgk(bl), writes=sgk(bl))

            i_s = 0
            idx = 0
            expand()
            while idx < len(items):
                while i_s <= idx + SKEW:
                    if i_s >= len(items):
                        if nxt_head["h"] < 16:
                            expand()
                        else:
                            break
                    emit_S(i_s)
                    i_s += 1
                emit_PV(idx)
                idx += 1
            if ats < 4:
                return
            for c in range(NCH):
                tbk = 2 + (c % 2)
                for bl in range(nblk):
                    P.op("pe", lambda e, c=c, bl=bl, tbk=tbk: e.transpose(out=PBb[tbk][:, bl * tb:(bl + 1) * tb],
                                                                       in_=sgt[0:tb, bl, c * 128:(c + 1) * 128], identity=identb[0:tb, 0:tb]),
                         reads=sgk(bl) + ["identb"], writes=bk(tbk))
                P.op("dve", lambda e, c=c, tbk=tbk: e.tensor_copy(out=actA[:, c, 0:T], in_=PBb[tbk][:, 0:T]), reads=bk(tbk), writes=[("A", c)])
            if ats < 5:
                return
            proj_residual(T, [34, 35], lambda kc: actA[:, kc, :], lambda kc: ("A", kc), 8, [0, 1])

        def run_final(T, tok0):
            nblk = max(1, T // 128)
            tb = min(128, T)
            ys = hv(0, 16, F32).rearrange("p (b f) -> p b f", f=1024)
            ysk = hk(0, 16)
            ytmp = [hv(16, 2, F32), hv(18, 2, F32)]
            ytk = [hk(16, 2), hk(18, 2)]
            for c in range(NCH):
                P.op("act", lambda e, c=c: e.activation(out=actA[:, c, 0:T], in_=x32[:, c, 0:T], func=AF.Square), reads=[("x", c)], writes=[("A", c)])
            mm_group(bk(0), PB[0][:, 0:T], [(onesb[:], actA[:, c, 0:T], ["onesb", ("A", c)]) for c in range(NCH)])
            P.op("act", lambda e: e.activation(out=rstd[:, 0:T], in_=PB[0][:, 0:T], func=AF.Ln, bias=epsb[:], scale=1.0 / D), reads=[("pb", 0), "epsb"], writes=["rstd"])
            P.op("act", lambda e: e.activation(out=rstd[:, 0:T], in_=rstd[:, 0:T], func=AF.Exp, scale=-0.5), reads=["rstd"], writes=["rstd"])
            for c in range(NCH):
                t = c % 2
                b = 1 + (c % 2)
                P.op("dve", lambda e, c=c, t=t: e.scalar_tensor_tensor(out=ytmp[t][:, 0:T], in0=x32[:, c, 0:T], scalar=vec[:, 40 + c:41 + c], in1=rstd[:, 0:T],
                                                                      op0=ALU.mult, op1=ALU.mult), reads=[("x", c), "vec", "rstd"], writes=ytk[t])
                for bl in range(nblk):
                    P.op("pe", lambda e, t=t, b=b, bl=bl: e.transpose(out=PB[b][0:tb, bl * 128:(bl + 1) * 128], in_=ytmp[t][:, bl * tb:(bl + 1) * tb], identity=ident32[:]),
                         reads=ytk[t] + ["ident32"], writes=[("pb", b)])
                P.op("act", lambda e, c=c, b=b: e.copy(out=ys[0:tb, 0:nblk, c * 128:(c + 1) * 128], in_=PB[b][0:tb, 0:nblk * 128].rearrange("p (b f) -> p b f", f=128)),
                     reads=[("pb", b)], writes=ysk)
            P.dma(STQ, "yo", lambda e: e.dma_start(out=y_out[tok0:tok0 + T, :].rearrange("(b p) f -> p b f", p=tb), in_=ys[0:tb, 0:nblk, :]),
                  reads=ysk, writes=[("out", len(out_keys))])
            out_keys.append(("out", len(out_keys)))

        tiles = []
        for s in range(NPS):
            for i in range(SEQ // TP):
                tiles.append((s, i, TP, s * SEQ + i * TP))
        tiles.append((NPS, 0, TS, NPS * SEQ))
        if tile_limit is not None:
            tiles = tiles[:tile_limit]

        KTv = KT[:].rearrange("p c (i t) -> p c i t", t=TP)
        Vcv = Vc[:].rearrange("p k (h d) -> p k h d", d=66)
        KTp = KT[:].rearrange("p (s c) t -> p s (c t)", c=2)
        KTn = sb("KTn", [128, 8, TS], BF16)
        Vn = sb("Vn", [TS, 16, 66], BF16)
        Vpv = Vc[:].rearrange("p (s k) f -> p s (k f)", k=2)

        for (s, i, T, tok0) in tiles:
            sample = (s == NPS)
            for c in range(NCH):
                P.dma("sp", ("x", c), lambda e, T=T, tok0=tok0, c=c: e.dma_start(out=x32[:, c, 0:T], in_=xT[c * 128:(c + 1) * 128, tok0:tok0 + T]),
                      writes=[("x", c)])
            if i == 0:
                for h in range(8):
                    S_par[h] = 0
                if sample:
                    P.dma("sp", "st0", lambda e: e.dma_start(out=S32[:, 0, :, :], in_=st0_src.rearrange("h k v -> k h v")),
                          writes=[("S32", 0, h) for h in range(8)])
                    for h in range(8):
                        P.op("act", lambda e, h=h: e.copy(out=Sbf[:, h, 0, :], in_=S32[:, 0, h, :]), reads=[("S32", 0, h)], writes=[("Sbf", h, 0)])
                else:
                    P.op("pool", lambda e: e.memset(S32[:, 0, :, :], 0.0), writes=[("S32", 0, h) for h in range(8)])
                    P.op("pool", lambda e: e.memset(Sbf[:, :, 0, :], 0.0), writes=[("Sbf", h, 0) for h in range(8)])
            if phase_limit < 1:
                continue
            P.tag = "hgrn"
            run_hgrn(T, i == 0, sample)
            if i == 0 and s == 0:
                dbg_dump("x_hgrn", lambda: x32[:].rearrange("p c t -> p (c t)"), [128, 4096], [("x", c) for c in range(NCH)])
            if (not sample and i == SEQ // TP - 1) or sample:
                po = S_par[0]
                P.dma(STQ, "sto", lambda e, s=s, po=po: e.dma_start(out=st_out[s].rearrange("h k v -> k h v"), in_=S32[:, po, :, :]),
                      reads=[("S32", po, h) for h in range(8)], writes=[("out", len(out_keys))])
                out_keys.append(("out", len(out_keys)))
            if phase_limit < 2:
                continue
            P.tag = "mlp0"
            norm(T, 8, actA, "A", 0)
            mlp(T, 0, actA, "A")
            if i == 0 and s == 0:
                dbg_dump("x_mlp0", lambda: x32[:].rearrange("p c t -> p (c t)"), [128, 4096], [("x", c) for c in range(NCH)])
            if phase_limit < 3:
                continue
            P.tag = "kv"
            if not sample:
                run_kv(T, tok0,
                       lambda pr, i=i: KTv[:, pr, i, :], lambda pr, i=i: [("KT", pr, i)],
                       lambda bl, i=i: Vcv[:, 4 * i + bl, :, :], lambda bl, i=i: [("V", 4 * i + bl)],
                       lambda bl, i=i: nGk[:, 4 * i + bl, :], "nGk", i == 0)

                def kbs_prompt(h, i=i):
                    pr, hh = divmod(h, 2)
                    base = hh * 64
                    out = []
                    for kb in range(4 * i + 4):
                        ti, kl = divmod(kb, 4)
                        out.append(dict(nk=128, diag=(kb - 4 * i if kb >= 4 * i else None),
                                        kt=KTv[:, pr, ti, kl * 128:(kl + 1) * 128], kt_keys=[("KT", pr, ti)],
                                        v=Vcv[:, kb, h, 0:65], v_keys=[("V", kb)],
                                        bias=nGk[:, kb, h:h + 1], bias_keys=["nGk"]))
                    return out
                if phase_limit >= 4:
                    P.tag = "attn"
                    run_attn(T, kbs_prompt, i)
            else:
                nGp = hv(30, 2, F32).rearrange("p (a b) -> p a b", b=16)
                lfp = hv(24, 2, F32).rearrange("p (a b) -> p a b", b=16)
                sfxA = hv(26, 2, F32).rearrange("p (a b) -> p a b", b=16)
                sfxB = hv(28, 2, F32).rearrange("p (a b) -> p a b", b=16)
                P.dma("sp", "lfp", lambda e: e.dma_start(out=lfp[:].rearrange("p a b -> p (a b)"), in_=lfp_src), writes=hk(24, 2))
                lfpv = lfp[:].rearrange("p a b -> p (a b)")
                P.op("pe", lambda e: e.matmul(PB[2][:, :], lhsT=SU32[:], rhs=lfpv, start=True, stop=True), reads=["SU32"] + hk(24, 2), writes=[("pb", 2)])
                P.op("pe", lambda e: e.matmul(PB[3][:, :], lhsT=ones32[:], rhs=lfpv, start=True, stop=True), reads=["ones32"] + hk(24, 2), writes=[("pb", 3)])
                P.op("dve", lambda e: e.tensor_copy(out=sfxA[:].rearrange("p a b -> p (a b)"), in_=PB[3][:, :]), reads=[("pb", 3)], writes=hk(26, 2))
                cur, nxt, ck, nk_ = sfxA, sfxB, hk(26, 2), hk(28, 2)
                for sh in (1, 2, 4, 8, 16):
                    P.op("dve", lambda e, cur=cur, nxt=nxt, sh=sh: e.tensor_tensor(out=nxt[:, 0:32 - sh, :], in0=cur[:, 0:32 - sh, :], in1=cur[:, sh:32, :], op=ALU.add),
                         reads=ck, writes=nk_)
                    P.op("dve", lambda e, cur=cur, nxt=nxt, sh=sh: e.tensor_copy(out=nxt[:, 32 - sh:32, :], in_=cur[:, 32 - sh:32, :]), reads=ck, writes=nk_)
                    cur, nxt, ck, nk_ = nxt, cur, nk_, ck
                P.op("dve", lambda e, cur=cur: e.tensor_tensor(out=nGp[:, 0:31, :], in0=PB[2][:, 0:31 * 16].rearrange("p (a b) -> p a b", b=16), in1=cur[:, 1:32, :], op=ALU.add),
                     reads=[("pb", 2)] + ck, writes=hk(30, 2))
                P.op("dve", lambda e: e.tensor_copy(out=nGp[:, 31, :], in_=PB[2][:, 31 * 16:32 * 16]), reads=[("pb", 2)], writes=hk(30, 2))
                P.op("pool", lambda e: e.memset(Vpv[:].rearrange("p s (k d) -> p (s k) d", d=66)[:, :, 64:65], 1.0),
                     reads=[("V", kb) for kb in range(16)], writes=[("V", kb) for kb in range(16)])
                P.op("pool", lambda e: e.memset(Vn[:, :, 64:65], 1.0), writes=["Vn"])
                run_kv(T, tok0,
                       lambda pr: KTn[:, pr, :], lambda pr: [("KTn", pr)],
                       lambda bl: Vn[:, :, :], lambda bl: ["Vn"],
                       lambda bl: nGk[0:TS, 0, :], "nGk", True)

                def kbs_sample(h):
                    pr, hh = divmod(h, 2)
                    base = hh * 64
                    ks = h % 2
                    vs = h % 8
                    ktk = [("KT", 2 * ks + cc, ii) for cc in range(2) for ii in range(4)]
                    vk = [("V", 2 * vs), ("V", 2 * vs + 1)]
                    P.dma("pool", ("ktp", ks), lambda e: e.dma_start(out=KTp[base:base + 64, ks, :].rearrange("p (a b) -> p a b", b=2048),
                                                                     in_=ktp_src[h].rearrange("p (a b) -> p a b", b=2048)), writes=ktk)
                    P.dma("pool", ("vp", vs), lambda e: e.dma_start(out=Vpv[:, vs, :].rearrange("p (k d) -> p k d", d=66)[:, :, 0:64],
                                                                    in_=vp_src[h].rearrange("p (k d) -> p k d", d=64)), writes=vk)
                    out = []
                    for kb in range(PAST // 128):
                        out.append(dict(nk=128, diag=None,
                                        kt=KTp[:, ks, kb * 128:(kb + 1) * 128], kt_keys=ktk,
                                        v=Vpv[:, vs, kb * 66:kb * 66 + 65], v_keys=vk,
                                        bias=nGp[:, kb, h:h + 1], bias_keys=hk(30, 2)))
                    out.append(dict(nk=TS, diag=0, kt=KTn[:, pr, :], kt_keys=[("KTn", pr)],
                                    v=Vn[:, h, 0:65], v_keys=["Vn"], bias=nGk[0:TS, 0, h:h + 1], bias_keys=["nGk"]))
                    return out
                if phase_limit >= 4:
                    P.tag = "attn"
                    run_attn(T, kbs_sample, 0)
            if phase_limit < 5:
                continue
            P.tag = "mlp1"
            norm(T, 32, actB, "B", 0)
            mlp(T, 1, actB, "B")
            P.tag = "final"
            run_final(T, tok0)

        P.op("sp", lambda e: e.nop(), reads=list(out_keys))
        P.analyze()
        lanes = P.lanes()
        sems = {l: es.enter_context(nc.semaphore("s%d" % i)) for i, l in enumerate(lanes)}
        print("[kernel] ops=%d lanes=%d waits=%d" % (len(P.ops), len(lanes), P.n_waits), flush=True)
        with nc.Block() as block:
            P.emit(sems, block)
    _LAST["P"] = P
    return nc


def _tile_w(W, cw):
    K, N = W.shape
    kc = K // 128
    out = []
    for j in range(N // cw):
        blk = W[:, j * cw:(j + 1) * cw].reshape(kc, 128, cw).transpose(1, 0, 2).reshape(128, kc * cw)
        out.append(blk)
    return out


def _pack_weights(w_in_a, w_o_a, w_up, w_down, w_kv, w_q_b, w_o_b):
    tiles = []
    wi = w_in_a[0]
    for h in range(8):
        cols = np.concatenate([wi[:, g * 1024 + h * 128: g * 1024 + (h + 1) * 128] for g in range(4)], axis=1)
        tiles += _tile_w(cols, 512)
    tiles += _tile_w(w_o_a[0], 512)
    tiles += _tile_w(w_up[0], 512)
    tiles += _tile_w(w_down[0], 128)
    tiles += _tile_w(w_kv[:, 0:2048], 512)
    tiles += _tile_w(w_q_b[0], 512)
    tiles += _tile_w(w_o_b[0], 512)
    tiles += _tile_w(w_up[1], 512)
    tiles += _tile_w(w_down[1], 128)
    assert len(tiles) == NWT
    return np.ascontiguousarray(np.stack(tiles, axis=0).astype(np.float32))


def _prepare(x_prompt, x_sample, state_hgrn, cache_k, cache_v, cache_logf, norm_a, w_in_a, lb_logits, g_norm_a, w_o_a,
             norm_kv, w_kv, b_f, norm_b, w_q_b, w_o_b, norm_mlp, w_up, w_down, norm_f, cores=range(8)):
    f32 = np.float32
    wsrc = _pack_weights(w_in_a, w_o_a, w_up, w_down, w_kv, w_q_b, w_o_b)
    wf = np.ascontiguousarray(w_kv[:, 2048:2064].reshape(8, 128, 16).transpose(1, 0, 2).reshape(128, 128))

    def fm(v):
        return v.reshape(8, 128).T

    vecs = np.concatenate([fm(norm_a[0]), fm(norm_mlp[0]), fm(norm_kv), fm(norm_b[0]), fm(norm_mlp[1]), fm(norm_f),
                           fm(lb_logits[0]), fm(lb_logits[1]), g_norm_a[0].reshape(128, 1)], axis=1)
    vecs = np.ascontiguousarray(vecs.astype(f32))
    in_maps = []
    for c in cores:
        xt = np.concatenate([x_prompt[2 * c].T, x_prompt[2 * c + 1].T, x_sample[c].T], axis=1)
        kT = cache_k[c].transpose(1, 2, 0)
        vp = cache_v[c].reshape(32, 128, 16, 64).transpose(2, 1, 0, 3).reshape(16, 128, 32 * 64)
        lp = cache_logf[c].reshape(32, 128, 16).transpose(1, 0, 2).reshape(128, 32 * 16)
        in_maps.append({
            "xT": np.ascontiguousarray(xt), "wsrc": wsrc, "wf": wf, "vecs": vecs,
            "b_f": np.ascontiguousarray(b_f.reshape(1, 16)),
            "state0": np.ascontiguousarray(state_hgrn[c, 0]),
            "kT_past": np.ascontiguousarray(kT), "v_past": np.ascontiguousarray(vp),
            "logf_past": np.ascontiguousarray(lp),
        })
    return in_maps


_NC_CACHE = {}
_LAST = {}


def kernel(x_prompt, x_sample, state_hgrn, cache_k, cache_v, cache_logf,
           norm_a, w_in_a, lb_logits, g_norm_a, w_o_a,
           norm_kv, w_kv, b_f, norm_b, w_q_b, w_o_b,
           norm_mlp, w_up, w_down, norm_f):
    f32 = np.float32
    args = [np.asarray(a, dtype=f32) for a in (x_prompt, x_sample, state_hgrn, cache_k, cache_v, cache_logf,
                                               norm_a, w_in_a, lb_logits, g_norm_a, w_o_a, norm_kv, w_kv, b_f,
                                               norm_b, w_q_b, w_o_b, norm_mlp, w_up, w_down, norm_f)]
    (x_prompt, x_sample, state_hgrn, cache_k, cache_v, cache_logf, norm_a, w_in_a, lb_logits, g_norm_a, w_o_a,
     norm_kv, w_kv, b_f, norm_b, w_q_b, w_o_b, norm_mlp, w_up, w_down, norm_f) = args
    in_maps = _prepare(x_prompt, x_sample, state_hgrn, cache_k, cache_v, cache_logf, norm_a, w_in_a, lb_logits, g_norm_a, w_o_a,
                       norm_kv, w_kv, b_f, norm_b, w_q_b, w_o_b, norm_mlp, w_up, w_down, norm_f)
    if "nc" not in _NC_CACHE:
        _NC_CACHE["nc"] = build()
    nc = _NC_CACHE["nc"]
    res = run_bass_kernel_spmd(nc, in_maps, core_ids=list(range(8)))
    R = res.results
    B, Bd = 16, 8
    y_p = np.empty((B, SEQ, D), f32); k_p = np.empty((B, SEQ, 16, 64), f32); v_p = np.empty((B, SEQ, 16, 64), f32)
    lf_p = np.empty((B, SEQ, 16), f32); st_p = np.empty((B, 1, 8, 128, 128), f32)
    y_s = np.empty((Bd, TS, D), f32); k_s = np.empty((Bd, TS, 16, 64), f32); v_s = np.empty((Bd, TS, 16, 64), f32)
    lf_s = np.empty((Bd, TS, 16), f32); st_s = np.empty((Bd, 1, 8, 128, 128), f32)
    for c in range(8):
        r = R[c]
        for s in range(2):
            b = 2 * c + s
            sl = slice(s * SEQ, (s + 1) * SEQ)
            y_p[b] = r["y"][sl]
            k_p[b] = r["ko"][sl].reshape(SEQ, 16, 64)
            v_p[b] = r["vo"][sl].reshape(SEQ, 16, 64)
            lf_p[b] = r["lfo"][sl]
            st_p[b, 0] = r["st"][s]
        sl = slice(2 * SEQ, 2 * SEQ + TS)
        y_s[c] = r["y"][sl]
        k_s[c] = r["ko"][sl].reshape(TS, 16, 64)
        v_s[c] = r["vo"][sl].reshape(TS, 16, 64)
        lf_s[c] = r["lfo"][sl]
        st_s[c, 0] = r["st"][2]
    return (y_p, y_s, st_p, k_p, v_p, lf_p, st_s, k_s, v_s, lf_s)
```
